# Optimizing a Trainium2 kernel written in Bass

```python
import jax, jax.numpy as jnp
from jax import lax
import numpy as np

D_MODEL = 1024
BATCH = 4
SEQ = 4096
DEPTH = 1

CHUNK = 64
LRU_WIDTH = 512
LRU_HEADS = 8
LRU_HEAD_DIM = LRU_WIDTH // LRU_HEADS
LRU_CONV_WIDTH = 4
LRU_C = 8.0
GMLP_WIDTH = 512
GMLP_GROUPS = 4
GMLP_GROUP_DIM = GMLP_WIDTH // GMLP_GROUPS
GMLP_BLOCK = 128
MIX_WIDTH = LRU_WIDTH + GMLP_WIDTH
IN_COLS = 2 * LRU_WIDTH + 2 * GMLP_WIDTH
D_FF = 3 * D_MODEL
FFN_CONV_WIDTH = 3
N_MOD = 6
EPS = 1e-6

kernel_name = "hybrid_rglru_gmlp_convffn_block"


def rmsnorm(x, g):
    xf = x.astype(jnp.float32)
    y = xf * lax.rsqrt(jnp.mean(xf * xf, axis=-1, keepdims=True) + EPS)
    return (y * g.astype(jnp.float32)).astype(x.dtype)


def layernorm(x, g, b):
    xf = x.astype(jnp.float32)
    mu = jnp.mean(xf, axis=-1, keepdims=True)
    var = jnp.mean(jnp.square(xf - mu), axis=-1, keepdims=True)
    y = (xf - mu) * lax.rsqrt(var + EPS)
    return (y * g.astype(jnp.float32) + b.astype(jnp.float32)).astype(x.dtype)


def causal_depthwise_conv(x, w, b):
    k_width = w.shape[0]
    s = x.shape[1]
    xp = jnp.pad(x, ((0, 0), (k_width - 1, 0), (0, 0)))
    out = xp[:, 0:s] * w[0]
    for k in range(1, k_width):
        out = out + xp[:, k:k + s] * w[k]
    return out + b


def _lin_rec_combine(left, right):
    a1, b1 = left
    a2, b2 = right
    return a1 * a2, a2 * b1 + b2


def rg_lru_group(x_raw, gate_raw, conv_w, conv_b, w_rgate, b_rgate, w_igate, b_igate, lru_a):
    bsz, s, _ = x_raw.shape
    xc = causal_depthwise_conv(x_raw, conv_w, conv_b)
    xh = xc.reshape(bsz, s, LRU_HEADS, LRU_HEAD_DIM)
    r = jax.nn.sigmoid(jnp.einsum('bshi,hij->bshj', xh, w_rgate) + b_rgate).reshape(bsz, s, LRU_WIDTH)
    i = jax.nn.sigmoid(jnp.einsum('bshi,hij->bshj', xh, w_igate) + b_igate).reshape(bsz, s, LRU_WIDTH)
    log_a = -LRU_C * r.astype(jnp.float32) * jax.nn.softplus(-lru_a.astype(jnp.float32))
    a = jnp.exp(log_a)
    mult = jnp.sqrt(-jnp.expm1(2.0 * log_a))
    bx = mult * (i * xc).astype(jnp.float32)
    _, h = lax.associative_scan(_lin_rec_combine, (a, bx), axis=1)
    return h.astype(x_raw.dtype) * jax.nn.gelu(gate_raw)


def gmlp_group(u_raw, v_raw, v_norm_g, v_norm_b, w_spatial, b_spatial):
    bsz, s, _ = u_raw.shape
    u = jax.nn.gelu(u_raw)
    v = layernorm(jax.nn.gelu(v_raw), v_norm_g, v_norm_b)
    vb = v.reshape(bsz, s // GMLP_BLOCK, GMLP_BLOCK, GMLP_GROUPS, GMLP_GROUP_DIM)
    pos = jnp.arange(GMLP_BLOCK)
    mask = (pos[None, :] // CHUNK) <= (pos[:, None] // CHUNK)
    ws = jnp.where(mask[None], w_spatial, jnp.zeros_like(w_spatial))
    sp = jnp.einsum('gij,bnjgc->bnigc', ws, vb) + b_spatial.T[None, None, :, :, None]
    return u * sp.reshape(bsz, s, GMLP_WIDTH)


def setup_inputs(seed: int = 0) -> dict:
    key = jax.random.key(seed)
    ks = jax.random.split(key, 32)
    L = DEPTH

    def nrm(k, shape, scale):
        return jax.random.normal(k, shape, jnp.float32) * scale

    x = nrm(ks[0], (BATCH, SEQ, D_MODEL), 1.0)
    c = nrm(ks[1], (BATCH, D_MODEL), 1.0)
    w_ada = nrm(ks[2], (L, D_MODEL, N_MOD * D_MODEL), 0.5 * D_MODEL ** -0.5)
    b_ada = nrm(ks[3], (L, N_MOD * D_MODEL), 0.02)
    g_mix_pre = 1.0 + nrm(ks[4], (L, D_MODEL), 0.02)
    g_mix_post = 1.0 + nrm(ks[5], (L, D_MODEL), 0.02)
    w_in = nrm(ks[6], (L, D_MODEL, IN_COLS), D_MODEL ** -0.5)
    conv_w = nrm(ks[7], (L, LRU_CONV_WIDTH, LRU_WIDTH), LRU_CONV_WIDTH ** -0.5)
    conv_b = nrm(ks[8], (L, LRU_WIDTH), 0.02)
    w_rgate = nrm(ks[9], (L, LRU_HEADS, LRU_HEAD_DIM, LRU_HEAD_DIM), LRU_HEAD_DIM ** -0.5)
    b_rgate = nrm(ks[10], (L, LRU_HEADS, LRU_HEAD_DIM), 0.02)
    w_igate = nrm(ks[11], (L, LRU_HEADS, LRU_HEAD_DIM, LRU_HEAD_DIM), LRU_HEAD_DIM ** -0.5)
    b_igate = nrm(ks[12], (L, LRU_HEADS, LRU_HEAD_DIM), 0.02)
    a_c = jax.random.uniform(ks[13], (L, LRU_WIDTH), jnp.float32, 0.9, 0.999)
    p = a_c ** (1.0 / LRU_C)
    lru_a = jnp.log(p) - jnp.log1p(-p)
    v_norm_g = 1.0 + nrm(ks[14], (L, GMLP_WIDTH), 0.02)
    v_norm_b = nrm(ks[15], (L, GMLP_WIDTH), 0.02)
    w_spatial = nrm(ks[16], (L, GMLP_GROUPS, GMLP_BLOCK, GMLP_BLOCK), GMLP_BLOCK ** -0.5)
    b_spatial = 1.0 + nrm(ks[17], (L, GMLP_GROUPS, GMLP_BLOCK), 0.02)
    g_lru_out = 1.0 + nrm(ks[18], (L, LRU_WIDTH), 0.02)
    g_gmlp_out = 1.0 + nrm(ks[19], (L, GMLP_WIDTH), 0.02)
    w_out = nrm(ks[20], (L, MIX_WIDTH, D_MODEL), MIX_WIDTH ** -0.5)
    g_ffn_pre = 1.0 + nrm(ks[21], (L, D_MODEL), 0.02)
    g_ffn_post = 1.0 + nrm(ks[22], (L, D_MODEL), 0.02)
    w_up = nrm(ks[23], (L, D_MODEL, 2 * D_FF), D_MODEL ** -0.5)
    ffn_conv_w = nrm(ks[24], (L, FFN_CONV_WIDTH, 2 * D_FF), FFN_CONV_WIDTH ** -0.5)
    ffn_conv_b = nrm(ks[25], (L, 2 * D_FF), 0.02)
    w_down = nrm(ks[26], (L, D_FF, D_MODEL), D_FF ** -0.5)
    return {"x": x, "c": c, "w_ada": w_ada, "b_ada": b_ada,
            "g_mix_pre": g_mix_pre, "g_mix_post": g_mix_post, "w_in": w_in,
            "conv_w": conv_w, "conv_b": conv_b, "w_rgate": w_rgate, "b_rgate": b_rgate,
            "w_igate": w_igate, "b_igate": b_igate, "lru_a": lru_a,
            "v_norm_g": v_norm_g, "v_norm_b": v_norm_b, "w_spatial": w_spatial, "b_spatial": b_spatial,
            "g_lru_out": g_lru_out, "g_gmlp_out": g_gmlp_out, "w_out": w_out,
            "g_ffn_pre": g_ffn_pre, "g_ffn_post": g_ffn_post, "w_up": w_up,
            "ffn_conv_w": ffn_conv_w, "ffn_conv_b": ffn_conv_b, "w_down": w_down}


def reference(x, c, w_ada, b_ada, g_mix_pre, g_mix_post, w_in, conv_w, conv_b,
              w_rgate, b_rgate, w_igate, b_igate, lru_a, v_norm_g, v_norm_b,
              w_spatial, b_spatial, g_lru_out, g_gmlp_out, w_out,
              g_ffn_pre, g_ffn_post, w_up, ffn_conv_w, ffn_conv_b, w_down):
    c_act = jax.nn.silu(c)
    for l in range(DEPTH):
        mod = c_act @ w_ada[l] + b_ada[l]
        sh_m, sc_m, gt_m, sh_f, sc_f, gt_f = [m[:, None, :] for m in jnp.split(mod, N_MOD, axis=-1)]

        h = rmsnorm(x, g_mix_pre[l]) * (1.0 + sc_m) + sh_m
        z = h @ w_in[l]
        lru_x, lru_gate, g_u, g_v = jnp.split(
            z, [LRU_WIDTH, 2 * LRU_WIDTH, 2 * LRU_WIDTH + GMLP_WIDTH], axis=-1)
        y_lru = rg_lru_group(lru_x, lru_gate, conv_w[l], conv_b[l], w_rgate[l], b_rgate[l],
                             w_igate[l], b_igate[l], lru_a[l])
        y_gmlp = gmlp_group(g_u, g_v, v_norm_g[l], v_norm_b[l], w_spatial[l], b_spatial[l])
        y = jnp.concatenate([rmsnorm(y_lru, g_lru_out[l]), rmsnorm(y_gmlp, g_gmlp_out[l])], axis=-1)
        y = y @ w_out[l]
        x = x + gt_m * rmsnorm(y, g_mix_post[l])

        h = rmsnorm(x, g_ffn_pre[l]) * (1.0 + sc_f) + sh_f
        up = causal_depthwise_conv(h @ w_up[l], ffn_conv_w[l], ffn_conv_b[l])
        g_ff, v_ff = jnp.split(up, 2, axis=-1)
        y = (jax.nn.gelu(g_ff) * v_ff) @ w_down[l]
        x = x + gt_f * rmsnorm(y, g_ffn_post[l])
    return x
```

```python
from contextlib import ExitStack

import numpy as np
import concourse.bass as bass
import concourse.mybir as mybir
from concourse.bass_utils import run_bass_kernel_spmd

F32 = mybir.dt.float32
BF16 = mybir.dt.bfloat16
AF = mybir.ActivationFunctionType
ALU = mybir.AluOpType

D = 1024
NTOK = 2048
CH = 512
NCHUNK = NTOK // CH
EPS = 1e-6
NPC = 232
AR = 16140


class Sem:
    def __init__(self, h):
        self.h = h
        self.count = 0


class Eng:
    def __init__(self, name, sem, is_pe=False):
        self.name = name
        self.sem = sem
        self.ops = []
        self.seen = {}
        self.is_pe = is_pe


class T:
    __slots__ = ("w", "r", "name")

    def __init__(self, name=""):
        self.w = None
        self.r = {}
        self.name = name


def _collect(E, reads, writes):
    needs = {}

    def need(s, v):
        if needs.get(s, 0) < v:
            needs[s] = v

    for t in reads:
        if t.w is not None:
            need(*t.w)
    for t in writes:
        if t.w is not None:
            need(*t.w)
        for s, v in t.r.items():
            need(s, v)
    waits = []
    for s, v in needs.items():
        if s is E.sem:
            if E.is_pe:
                continue
            if v < E.sem.count - 3:
                continue
        if E.seen.get(s, 0) >= v:
            continue
        E.seen[s] = v
        waits.append((s, v))
    return waits


def emit(E, fn, reads=(), writes=(), inc=True):
    waits = _collect(E, reads, writes)
    if inc:
        E.sem.count += 1
        idx = E.sem.count
    else:
        idx = E.sem.count + 1
    E.ops.append((waits, fn, E.sem if inc else None, 1))
    dep = (E.sem, idx)
    for t in writes:
        t.w = dep
        t.r = {}
    for t in reads:
        if t in writes:
            continue
        if t.r.get(E.sem, 0) < idx:
            t.r[E.sem] = idx


def emit_dma(Q, sem, fn, reads=(), writes=()):
    waits = _collect(Q, reads, writes)
    if sem.count > 0 and Q.seen.get(sem, 0) < sem.count:
        Q.seen[sem] = sem.count
        waits.append((sem, sem.count))
    sem.count += 16
    Q.ops.append((waits, fn, sem, 16))
    dep = (sem, sem.count)
    for t in writes:
        t.w = dep
        t.r = {}
    for t in reads:
        if t in writes:
            continue
        if t.r.get(sem, 0) < sem.count:
            t.r[sem] = sem.count


def replay(E, eng):
    for waits, fn, sem, amt in E.ops:
        for s, v in waits:
            eng.wait_ge(s.h, v)
        ins = fn(eng)
        if sem is not None:
            ins.then_inc(sem.h, amt)


def build_nc():
    nc = bass.Bass("TRN2", target_bir_lowering=False)

    def din(name, shape):
        return nc.dram_tensor(name, shape, F32, kind="ExternalInput").ap()

    x_d = din("x", [NTOK, D])
    xp_d = din("xp", [NTOK, D])
    flag_d = din("flag", [128, 1])
    cvec_d = din("cvec", [128, 8])
    w_ada_d = din("w_ada", [D, 6 * D])
    b_ada_d = din("b_ada", [6 * D])
    gpre_d = [din(n, [D]) for n in ("g_mix_pre", "g_mix_post", "g_ffn_pre", "g_ffn_post")]
    w_in_d = din("w_in", [D, 2048])
    pcols_d = din("pcols", [128, NPC])
    wrg_d = din("w_rgate", [128, 4, 128])
    wig_d = din("w_igate", [128, 4, 128])
    vng_d = din("v_norm_g", [512])
    vnb_d = din("v_norm_b", [512])
    wsT_d = din("wsT", [4, 128, 128])
    bsp_d = din("b_spatial", [1, 512])
    w_out_d = din("w_out", [D, D])
    w_up_d = din("w_up", [D, 6144])
    w_down_d = din("w_down", [3072, D])
    out_d = nc.dram_tensor("out", [NTOK, D], F32, kind="ExternalOutput").ap()
    wsc_d = nc.dram_tensor("wup_bf", [24, 128, 2048], BF16, kind="Internal").ap()

    with ExitStack() as es:
        def S(name, shape, dt):
            return es.enter_context(nc.sbuf_tensor(name, shape, dt))

        def PSt(name, shape, dt):
            return es.enter_context(nc.psum_tensor(name, shape, dt))

        def SEM(name):
            return Sem(es.enter_context(nc.semaphore(name)))

        PE = Eng("pe", SEM("s_pe"), is_pe=True)
        A = Eng("act", SEM("s_act"))
        V = Eng("dve", SEM("s_dve"))
        G = Eng("pool", SEM("s_pool"))
        Q = Eng("sp", SEM("s_sp"))
        engines = [PE, A, V, G, Q]

        pc = S("pc", [128, NPC], F32)
        T_pc = T("pc")
        flag_sb = S("flag_sb", [128, 1], F32)
        T_flag = T()
        small = S("small", [128, 128], F32)
        ident = S("ident", [128, 128], BF16)
        ones_bf = S("ones_bf", [128, 128], BF16)
        T_const = T()
        wr_bd = S("wr_bd", [128, 4, 128], BF16)
        wi_bd = S("wi_bd", [128, 4, 128], BF16)
        T_gw = T()
        T_gw2 = T()
        wsT_sb = S("wsT_sb", [128, 4, 128], BF16)
        T_ws = T()
        bsp_bf = S("bsp_bf", [1, 512], BF16)
        T_bsp = T()
        MOD = [S(f"mod{n}", [128, D], F32) for n in range(6)]
        T_mod = [T() for _ in range(6)]
        MSH_m, MG_m, MGP_m, MSH_f, MG_f, MGP_f = MOD
        TSH_m, TG_m, TGP_m, TSH_f, TG_f, TGP_f = T_mod
        WMIX = S("wmix", [128, 24576], BF16)
        T_win = [T() for _ in range(4)]
        T_wouth = [T(), T()]
        W_in_sb = WMIX[:, 0:16384].rearrange("p (k n) -> p k n", k=8)
        W_out_sb = WMIX[:, 16384:24576].rearrange("p (k n) -> p k n", k=8)
        W_dn_sb = WMIX[:, :].rearrange("p (j n) -> p j n", j=24)
        XB = S("xb", [128, 16, D], F32)
        T_xb = [T(f"xb{i}") for i in range(16)]
        halo_f = S("halo_f", [128, 48, 2], F32)
        T_halo_f = [T() for _ in range(48)]
        h2halo = S("h2halo", [128, 8, 2], BF16)
        T_h2halo = T()
        arena = S("arena", [128, AR], F32)

        _so = [0]

        def sm(n):
            ap = small[:, _so[0]:_so[0] + n]
            _so[0] += n
            assert _so[0] <= 128
            return ap

        ss4 = sm(4); T_ss4 = T()
        rstd4 = sm(4); T_rstd4 = T()
        negc = sm(4); negc_h = sm(4)
        brh = sm(4); bih = sm(4)
        T_lp = T()
        carry = sm(4); T_carry = [T() for _ in range(4)]
        halo_x = sm(12).rearrange("p (j c) -> p j c", j=4); T_halo_x = [T() for _ in range(4)]
        lnst = sm(6); T_lnst = T()
        lnmv = sm(2); T_lnmv = T()
        lnr = sm(1); T_lnr = T()
        rl4 = sm(4); T_rl4 = T()
        rg4 = sm(4); T_rg4 = T()
        ss2 = sm(1); T_ss2 = T()
        rs2 = sm(1); T_rs2 = T()
        stmp = sm(16); T_stmp = T()
        cact_f = sm(8); T_cactf = T()
        eps_t = sm(1); one_t = sm(1); T_k = T()
        sl4 = sm(4); sg4 = sm(4); T_sl4 = T()
        lnmv4 = sm(8).rearrange("p (t c) -> p t c", t=4); lnr4 = sm(4)

        def pcc(off, n=1):
            return pc[:, off:off + n]

        PB = [PSt(f"pb{i}", [128, 512], F32) for i in range(7)]
        T_pb = [T(f"pb{i}") for i in range(7)]
        TRp = PSt("trp", [128, 1024], BF16)
        T_tr = T("tr")

        xsem = [SEM(f"xs{i}") for i in range(4)]
        wsem = [SEM(f"ws{i}") for i in range(4)]
        msem = [SEM(f"ms{i}") for i in range(4)]
        osem = [SEM(f"os{i}") for i in range(4)]
        csem = [SEM(f"cs{i}") for i in range(8)]
        usem = [SEM(f"us{i}") for i in range(6)]
        ssem = [SEM(f"ss{i}") for i in range(20)]
        _rr = {"x": 0, "w": 0, "m": 0, "o": 0, "c": 0, "u": 0, "s": 0}

        def nxt(kind, lst):
            i = _rr[kind]
            _rr[kind] = (i + 1) % len(lst)
            return lst[i]

        def qdma(out, in_, reads=(), writes=()):
            emit_dma(Q, nxt("m", msem), lambda e, o=out, i=in_: e.dma_start(out=o, in_=i),
                     reads, writes)

        def gdma(out, in_, reads=(), writes=(), small=False):
            emit_dma(G, nxt("s", ssem) if small else nxt("w", wsem),
                     lambda e, o=out, i=in_: e.dma_start(out=o, in_=i), reads, writes)

        def act(out, in_, func, reads, writes, bias=None, scale=None, accum=None, eng=None):
            kw = {}
            if bias is not None:
                kw["bias"] = bias
            if scale is not None:
                kw["scale"] = scale
            if accum is not None:
                kw["accum_out"] = accum
            emit(A, lambda e: e.activation(out=out, in_=in_, func=func, **kw), reads, writes)

        def ts(E, out, in0, s1, s2, op0, op1, reads, writes):
            if s2 is None:
                emit(E, lambda e: e.tensor_scalar(out=out, in0=in0, scalar1=s1, scalar2=None, op0=op0),
                     reads, writes)
            else:
                emit(E, lambda e: e.tensor_scalar(out=out, in0=in0, scalar1=s1, scalar2=s2, op0=op0, op1=op1),
                     reads, writes)

        def stt(E, out, in0, scalar, in1, op0, op1, reads, writes):
            emit(E, lambda e: e.scalar_tensor_tensor(out=out, in0=in0, scalar=scalar, in1=in1, op0=op0, op1=op1),
                 reads, writes)

        def tt_(E, out, in0, in1, op, reads, writes):
            emit(E, lambda e: e.tensor_tensor(out=out, in0=in0, in1=in1, op=op), reads, writes)

        def cp(E, out, in_, reads, writes):
            emit(E, lambda e: e.tensor_copy(out=out, in_=in_), reads, writes)

        def rsqrt(out, in_, scale, reads, Tout):
            act(out, in_, AF.Sqrt, list(reads) + [T_k], [Tout], bias=eps_t[:, 0:1], scale=scale)
            emit(V, lambda e: e.reciprocal(out=out, in_=out), [Tout], [Tout])

        def mm(out, lhsT, rhs, start, stop, reads, writes, inc=None):
            if inc is None:
                inc = stop
            emit(PE, lambda e: e.matmul(out, lhsT, rhs, start=start, stop=stop), reads, writes, inc=inc)

        def barrier():
            allsems = [E.sem for E in engines] + xsem + wsem + msem + osem + csem + usem + ssem
            for E in engines:
                waits = []
                for s in allsems:
                    if s is E.sem or s.count == 0:
                        continue
                    if E.seen.get(s, 0) < s.count:
                        E.seen[s] = s.count
                        waits.append((s, s.count))
                if waits:
                    E.ops.append((waits, lambda e: None, None, 0))

        class Carve:
            def __init__(self):
                self.off = 0

            def f32(self, n):
                ap = arena[:, self.off:self.off + n]
                self.off += n
                assert self.off <= AR, self.off
                return ap

            def bf16(self, n):
                assert n % 2 == 0
                ap = arena[:, self.off:self.off + n // 2].bitcast(BF16)
                self.off += n // 2
                assert self.off <= AR, self.off
                return ap

        def _mk_slot(t0):
            return (XB[:, t0:t0 + 2, :].rearrange("p a b -> p (a b)").bitcast(BF16).rearrange("p (k n) -> p k n", k=8),
                    [T_xb[t0], T_xb[t0 + 1]])
        wa_sl = {"m": [_mk_slot(12), _mk_slot(14)], "f": [_mk_slot(0), _mk_slot(2)]}
        gtmp = [XB[:, 8, :], XB[:, 9, :], XB[:, 4, :], XB[:, 5, :]]
        T_gtmp = [T_xb[8], T_xb[9], T_xb[4], T_xb[5]]
        cact_bc = S("cact_bc", [128, 8, 128], BF16)
        T_cact = T()

        emit(G, lambda e: e.memset(small[:, :], 0.0), writes=[T_carry[0], T_carry[1], T_carry[2], T_carry[3],
                                                              T_halo_x[0], T_halo_x[1], T_halo_x[2], T_halo_x[3],
                                                              T_cactf, T_lp, T_ss4, T_rstd4, T_k])
        emit(G, lambda e: e.memset(halo_f[:, :, :], 0.0), writes=T_halo_f)
        emit(G, lambda e: e.memset(eps_t, EPS), writes=[T_k])
        emit(G, lambda e: e.memset(one_t, 1.0), writes=[T_k])
        qdma(pc[:, :], pcols_d[:, :], writes=[T_pc])
        qdma(flag_sb[:, :], flag_d[:, :], writes=[T_flag])
        qdma(cact_f, cvec_d[:, :], writes=[T_cactf])

        emit(G, lambda e: e.memset(ident[:, :], 0.0), writes=[T_const])
        emit(G, lambda e: e.affine_select(out=ident[:, :], in_=ident[:, :], compare_op=ALU.not_equal, fill=1.0,
                                          base=0, pattern=[[-1, 128]], channel_multiplier=1), writes=[T_const])
        emit(G, lambda e: e.memset(ones_bf[:, :], 1.0), writes=[T_const])

        wada_v = w_ada_d.rearrange("(k p) n -> p k n", p=128)

        def load_wa(b):
            slot, Tsl = wa_sl["m" if b < 6 else "f"][b % 2]
            n, half = b // 2, b % 2
            gdma(slot, wada_v[:, :, b * 512:(b + 1) * 512], writes=Tsl)
            qdma(MOD[n][:, half * 512:(half + 1) * 512], b_ada_d[b * 512:(b + 1) * 512].partition_broadcast(128),
                 writes=[T_modh[n][half]])

        def comp_wa(b, pbk=None):
            n, half = b // 2, b % 2
            slot, Tsl = wa_sl["m" if b < 6 else "f"][b % 2]
            if pbk is None:
                pbk = b % 2
            for k in range(8):
                mm(PB[pbk][:, :], cact_bc[:, k, :], slot[:, k, :], k == 0, k == 7,
                   [T_cact, *Tsl], [T_pb[pbk]])
            mh = MOD[n][:, half * 512:(half + 1) * 512]
            tt_(V, mh, PB[pbk][:, :], mh, ALU.add, [T_pb[pbk], T_modh[n][half]], [T_modh[n][half], T_mod[n]])

        T_modh = [[T(), T()] for _ in range(6)]
        load_wa(0)
        load_wa(1)
        w_in_v = w_in_d.rearrange("(k p) n -> p k n", p=128)
        gdma(W_in_sb[:, :, 0:512], w_in_v[:, :, 0:512], writes=[T_win[0]])
        act(cact_f, cact_f, AF.Silu, [T_cactf], [T_cactf])
        cp(V, cact_bc[:, :, :], cact_f.unsqueeze(2).to_broadcast([128, 8, 128]), [T_cactf], [T_cact])

        la = pcc(28, 4)
        x_ = stmp[:, 0:4]; ax = stmp[:, 4:8]; ee = stmp[:, 8:12]; mx = stmp[:, 12:16]
        ts(V, x_, la, -1.0, None, ALU.mult, None, [T_pc], [T_stmp])
        tt_(V, ax, x_, la, ALU.max, [T_stmp, T_pc], [T_stmp])
        ts(V, mx, x_, 0.0, None, ALU.max, None, [T_stmp], [T_stmp])
        act(ee, ax, AF.Exp, [T_stmp], [T_stmp], scale=-1.0)
        act(ee, ee, AF.Ln, [T_stmp, T_k], [T_stmp], bias=one_t[:, 0:1])
        tt_(V, ee, ee, mx, ALU.add, [T_stmp], [T_stmp])
        ts(V, negc, ee, -8.0, None, ALU.mult, None, [T_stmp], [T_lp])
        ts(V, negc_h, ee, -4.0, None, ALU.mult, None, [T_stmp], [T_lp])
        ts(V, brh, pcc(20, 4), 0.5, None, ALU.mult, None, [T_pc], [T_lp])
        ts(V, bih, pcc(24, 4), 0.5, None, ALU.mult, None, [T_pc], [T_lp])

        w_out_v = w_out_d.rearrange("(k p) n -> p k n", p=128)

        def _wout_task(hh):
            def f():
                qdma(XB[:, 8:12, :], w_out_v[:, hh * 4:(hh + 1) * 4, :], writes=T_xb[8:12])
                for kk in range(4):
                    k = hh * 4 + kk
                    ts(V, W_out_sb[:, k, :], XB[:, 8 + kk, :], pcc(32 + k), None, ALU.mult, None,
                       [T_xb[8 + kk], T_pc], [T_wouth[hh]])
            return f

        qdma(gtmp[0], gpre_d[0].partition_broadcast(128), writes=[T_gtmp[0]])

        def _gpost_load():
            qdma(gtmp[1], gpre_d[1].partition_broadcast(128), writes=[T_gtmp[1]])

        for b in range(4):
            comp_wa(b)
            load_wa(b + 2)
        stt(V, MG_m[:, :], MG_m[:, :], 1.0, gtmp[0], ALU.add, ALU.mult, [TG_m, T_gtmp[0]], [TG_m])
        gdma(wr_bd[:, :, :], wrg_d[:, :, :], writes=[T_gw], small=True)
        gdma(wi_bd[:, :, :], wig_d[:, :, :], writes=[T_gw2], small=True)
        gdma(wsT_sb[:, :, :], wsT_d.rearrange("g j i -> j g i"), writes=[T_ws], small=True)
        emit(G, lambda e: e.memset(wsT_sb[64:128, :, 0:64], 0.0), writes=[T_ws])
        gdma(bsp_bf[:, :], bsp_d[:, :], writes=[T_bsp], small=True)

        def _win_rest():
            for q in range(1, 4):
                gdma(W_in_sb[:, :, q * 512:(q + 1) * 512], w_in_v[:, :, q * 512:(q + 1) * 512], writes=[T_win[q]])


        def _mtask(b):
            def f():
                comp_wa(b, pbk=2)
            return f

        def _m_final():
            tt_(V, MGP_m[:, :], MGP_m[:, :], gtmp[1], ALU.mult, [TGP_m, T_gtmp[1]], [TGP_m])

        def _f_start():
            load_wa(6)
            load_wa(7)

        def _ftask(b):
            def f():
                comp_wa(b, pbk=2)
                if b + 2 < 12:
                    load_wa(b + 2)
            return f

        def _f_gtmp():
            for n in (2, 3):
                qdma(gtmp[n], gpre_d[n].partition_broadcast(128), writes=[T_gtmp[n]])

        def _f_final():
            stt(V, MG_f[:, :], MG_f[:, :], 1.0, gtmp[2], ALU.add, ALU.mult, [TG_f, T_gtmp[2]], [TG_f])
            tt_(V, MGP_f[:, :], MGP_f[:, :], gtmp[3], ALU.mult, [TGP_f, T_gtmp[3]], [TGP_f])

        bg_by_chunk = {
            0: [_gpost_load, _mtask(4), _mtask(5), _m_final, _wout_task(0), _wout_task(1), _win_rest],
            1: [_f_start, _ftask(6), _ftask(7), _ftask(8)],
            2: [_f_gtmp, _ftask(9), _ftask(10), _ftask(11), _f_final],
        }
        cur_chunk = [0]

        _w = [(sm_, sm_.count) for sm_ in ssem if sm_.count > 0 and PE.seen.get(sm_, 0) < sm_.count]
        for sm_, c_ in _w:
            PE.seen[sm_] = c_
        PE.ops.append((_w, lambda e: None, None, 0))

        cv = Carve()
        vng = cv.f32(512); vnb = cv.f32(512); T_vn = T(); T_vn2 = T()
        t1 = cv.f32(D); T_t1 = T("t1")
        zx = cv.f32(516); T_zx = T("zx")
        xc = [cv.f32(512), cv.f32(512)]; T_xc = [T(), T()]
        reg_hs = cv.f32(2048)
        hs = reg_hs.rearrange("p (j n) -> p j n", j=4); T_hs = [T() for _ in range(4)]
        r01 = [cv.f32(512), cv.f32(512)]
        i01 = [cv.f32(512), cv.f32(512)]
        gg = [cv.f32(512) for _ in range(2)]; T_gg = [T(), T()]
        to1 = cv.f32(512); T_to1 = T()
        hb = cv.bf16(1024); T_hb = T("hb")
        hbs = [(hb, T_hb), (cv.bf16(1024), T("hb2"))]
        hT = cv.bf16(4096).rearrange("p (k n) -> p k n", k=8); T_hT = T("hT")
        xcb = cv.bf16(512); T_xcb = T()
        reg_ylb = cv.f32(1024); reg_ygb = cv.f32(1024); reg_vf = cv.f32(1024)
        ylb = reg_ylb.bitcast(BF16).rearrange("p (j n) -> p j n", j=4); T_ylb = [T() for _ in range(4)]
        ygb = reg_ygb.bitcast(BF16).rearrange("p (j n) -> p j n", j=4); T_ygb = [T() for _ in range(4)]
        vfull = reg_vf.bitcast(BF16).rearrange("p (t n) -> p t n", t=4); T_vf = [T() for _ in range(4)]
        ysq2 = cv.bf16(1024); T_ysq = [T(), T()]
        ysq = [ysq2[:, 0:512], ysq2[:, 512:1024]]
        r_ = [r01[0], r01[1], reg_ylb[:, 0:512], reg_ylb[:, 512:1024]]
        T_r = [[T()], [T()], [T_ylb[0], T_ylb[1]], [T_ylb[2], T_ylb[3]]]
        i_ = [i01[0], i01[1], reg_ygb[:, 0:512], reg_ygb[:, 512:1024]]
        T_i = [[T()], [T()], [T_ygb[0], T_ygb[1]], [T_ygb[2], T_ygb[3]]]
        zxs = [zx, reg_vf[:, 0:516]]
        T_zxs = [[T_zx], [T_vf[0], T_vf[1], T_vf[2]]]
        ybufs = [(t1, [T_t1]), (reg_hs[:, 0:1024], [T_hs[0], T_hs[1]]), (reg_hs[:, 1024:2048], [T_hs[2], T_hs[3]])]
        tbufs = [(to1, [T_to1]), (r01[0], T_r[0]), (r01[1], T_r[1]), (i01[0], T_i[0]), (i01[1], T_i[1])]
        ss2c = sm(4); rs2c = sm(4); T_ss2c = [T() for _ in range(4)]; T_rs2c = [T() for _ in range(4)]
        _yb = [0]
        _tb = [0]
        gbufs = [(gg[0], [T_gg[0]]), (gg[1], [T_gg[1]]), (r01[0], T_r[0]), (r01[1], T_r[1]),
                 (i01[0], T_i[0]), (i01[1], T_i[1])]
        _gb = [0]

        def gbuf():
            b = gbufs[_gb[0] % len(gbufs)]
            _gb[0] += 1
            return b

        qdma(vng, vng_d.partition_broadcast(128), writes=[T_vn])
        qdma(vnb, vnb_d.partition_broadcast(128), writes=[T_vn2])

        ZB = [0, 1, 3, 4, 5, 6]
        SSB = 2
        OB = [(3, 4), (5, 6)]
        _zr = [0]

        def zbank():
            b = ZB[_zr[0] % len(ZB)]
            _zr[0] += 1
            return b

        def load_x(src, row0, tile0):
            sem = nxt("x", xsem)
            emit_dma(Q, sem,
                     lambda e: e.dma_start(out=XB[:, tile0:tile0 + 4, :],
                                           in_=src[row0:row0 + CH, :].rearrange("(t p) d -> p t d", p=128)),
                     writes=T_xb[tile0:tile0 + 4])

        def rms_to_hT(tile_idx, tcol, MGt, TGt, MSHt, TSHt, dstT, T_dst, ncol):
            xt = XB[:, tile_idx, :]
            tb, Ttb = ybufs[_yb[0] % 3]
            _yb[0] += 1
            stt(V, tb, xt, rstd4[:, tcol:tcol + 1], MGt[:, :], ALU.mult, ALU.mult,
                [T_xb[tile_idx], T_rstd4, TGt], Ttb)
            hbx, Thbx = hbs[tcol % 2]
            tt_(G if tcol % 2 == 0 else V, hbx, tb, MSHt[:, :], ALU.add, [*Ttb, TSHt], [Thbx])
            for k in range(8):
                emit(PE, lambda e, k=k, hb=hbx: e.transpose(out=TRp[:, k * 128:(k + 1) * 128],
                                                             in_=hb[:, k * 128:(k + 1) * 128], identity=ident[:, :]),
                     [Thbx, T_const], [T_tr], inc=(k == 7))
            act(dstT[:, :, tcol * 128:(tcol + 1) * 128], TRp[:, :].rearrange("p (k n) -> p k n", k=8), AF.Copy,
                [T_tr], [T_dst])

        def rms_stats(tile0, ntile):
            for t in range(ntile):
                act(hbs[t % 2][0], XB[:, tile0 + t, :], AF.Square, [T_xb[tile0 + t]], [hbs[t % 2][1], T_ss4],
                    accum=ss4[:, t:t + 1])
            rsqrt(rstd4[:, 0:ntile], ss4[:, 0:ntile], 1.0 / D, [T_ss4], T_rstd4)

        def mix_chunk(tile0, full, p3=False):
            rms_stats(tile0, 4)
            for t in range(4):
                rms_to_hT(tile0 + t, t, MG_m, TG_m, MSH_m, TSH_m, hT, T_hT, 512)
            ZXB = [0, 1, 3, 4]
            RIB = [(5, 6), (0, 1), (5, 6), (3, 4)]
            for j in range(4):
                zb = ZXB[j]
                for k in range(8):
                    mm(PB[zb][:, :], W_in_sb[:, k, j * 128:(j + 1) * 128], hT[:, k, :], k == 0, k == 7,
                       [T_win[0], T_hT], [T_pb[zb]])

            def s1(j):
                pump()
                zb = ZXB[j]
                zxj = zxs[j % 2]; Tzx = T_zxs[j % 2]
                cp(G, zxj[:, 0:3], halo_x[:, j, :], [T_halo_x[j]], Tzx)
                act(zxj[:, 3:515], PB[zb][:, :], AF.Copy, [T_pb[zb]], Tzx)
                cp(G, halo_x[:, j, :], zxj[:, 512:515], Tzx, [T_halo_x[j]])
                xcj = xc[j % 2]; Txc = T_xc[j % 2]
                ts(V, xcj, zxj[:, 3:515], pcc(3 * 4 + j), pcc(16 + j), ALU.mult, ALU.add, [*Tzx, T_pc], [Txc])
                for kk in (2, 1, 0):
                    stt(V, xcj, zxj[:, kk:kk + 512], pcc(kk * 4 + j), xcj, ALU.mult, ALU.add,
                        [*Tzx, T_pc, Txc], [Txc])
                cp(G, xcb, xcj, [Txc], [T_xcb])
                zr, zi = RIB[j]
                mm(PB[zr][:, :], wr_bd[:, j, :], xcb, True, True, [T_gw, T_xcb], [T_pb[zr]])
                mm(PB[zi][:, :], wi_bd[:, j, :], xcb, True, True, [T_gw2, T_xcb], [T_pb[zi]])
                return zr, zi

            def s2a(j, zr, zi):
                pump()
                xcj = xc[j % 2]; Txc = T_xc[j % 2]
                rj = r_[j]; ij = i_[j]; Tr_ = T_r[j]; Ti_ = T_i[j]
                act(rj, PB[zr][:, :], AF.Tanh, [T_pb[zr], T_lp], Tr_, bias=brh[:, j:j + 1], scale=0.5)
                act(ij, PB[zi][:, :], AF.Tanh, [T_pb[zi], T_lp], Ti_, bias=bih[:, j:j + 1], scale=0.5)
                act(hs[:, j, :], rj, AF.Exp, [*Tr_, T_lp], [T_hs[j]], bias=negc[:, j:j + 1], scale=negc[:, j:j + 1])
                act(rj, rj, AF.Exp, [*Tr_, T_lp], Tr_, bias=negc_h[:, j:j + 1], scale=negc_h[:, j:j + 1])
                stt(V, ij, ij, 1.0, xcj, ALU.add, ALU.mult, [*Ti_, Txc], Ti_)

            def s2b(j):
                rj = r_[j]; ij = i_[j]; Tr_ = T_r[j]; Ti_ = T_i[j]
                act(hs[:, j, :], hs[:, j, :], AF.Sqrt, [T_hs[j], T_k], [T_hs[j]], bias=one_t[:, 0:1], scale=-1.0)
                stt(V, ij, ij, 0.5, hs[:, j, :], ALU.mult, ALU.mult, [*Ti_, T_hs[j]], Ti_)
                emit(V, lambda e, j=j, hs=hs, rj=rj, ij=ij: e.tensor_tensor_scan(
                    out=hs[:, j, :], data0=rj, data1=ij, initial=carry[:, j:j + 1], op0=ALU.mult, op1=ALU.add),
                     [*Tr_, *Ti_, T_carry[j]], [T_hs[j]])
                cp(G, carry[:, j:j + 1], hs[:, j, 511:512], [T_hs[j]], [T_carry[j]])

            zbk = {}
            for jp in (0, 2):
                zbk[jp] = s1(jp)
                zbk[jp + 1] = s1(jp + 1)
                if jp == 2:
                    s2b(0)
                    s2b(1)
                s2a(jp, *zbk[jp])
                s2a(jp + 1, *zbk[jp + 1])
            s2b(2)
            s2b(3)
            if not full:
                return

            TS = [3] if p3 else [0, 1, 2, 3]
            cs_ = slice(384, 512) if p3 else slice(0, 512)

            def ss_mm(jj, base):
                yq = ysq[jj % 2]; Tyq = T_ysq[jj % 2]
                for t in TS:
                    c0 = base + jj * 4 + t
                    mm(PB[SSB][:, c0:c0 + 1], yq[:, t * 128:(t + 1) * 128], ones_bf[:, 0:1],
                       True, True, [Tyq, T_const], [T_pb[SSB]], inc=(t == 3))

            zgb = []
            for j in range(4):
                zb = zbank(); zgb.append(zb)
                for k in range(8):
                    mm(PB[zb][:, :], W_in_sb[:, k, 512 + j * 128:512 + (j + 1) * 128], hT[:, k, :], k == 0, k == 7,
                       [T_win[1], T_hT], [T_pb[zb]])

            def zv_mm(t):
                zb = zbank(); zvb[t] = zb
                for k in range(8):
                    mm(PB[zb][:, :], hT[:, k, t * 128:(t + 1) * 128], W_in_sb[:, k, 1536:2048], k == 0, k == 7,
                       [T_win[3], T_hT], [T_pb[zb]])

            zvb = {}
            for t in TS[:2]:
                zv_mm(t)
            gl_ = []
            for j in range(4):
                g_, Tg = gbuf()
                gl_.append((g_, Tg))
                act(g_[:, cs_], PB[zgb[j]][:, cs_], AF.Gelu_apprx_tanh, [T_pb[zgb[j]]], Tg)
            for j in range(4):
                g_, Tg = gl_[j]
                tt_(V, ylb[:, j, cs_], hs[:, j, cs_], g_[:, cs_], ALU.mult, [T_hs[j], *Tg], [T_ylb[j]])
            for j in range(4):
                act(ysq[j % 2][:, cs_], ylb[:, j, cs_], AF.Square, [T_ylb[j]], [T_ysq[j % 2]])
                ss_mm(j, 0)
            for t in TS[2:]:
                zv_mm(t)
            for t in TS:
                act(hs[:, t, :], PB[zvb[t]][:, :], AF.Gelu_apprx_tanh, [T_pb[zvb[t]]], [T_hs[t]])
                emit(V, lambda e, t=t, hs=hs: e.bn_stats(out=lnst, in_=hs[:, t, :]), [T_hs[t]], [T_lnst])
                emit(V, lambda e, t=t: e.bn_aggr(out=lnmv4[:, t, :], in_=lnst), [T_lnst], [T_lnmv])
            zub = []
            for g in range(4):
                zb = zbank(); zub.append(zb)
                for k in range(8):
                    mm(PB[zb][:, :], W_in_sb[:, k, 1024 + g * 128:1024 + (g + 1) * 128], hT[:, k, :], k == 0, k == 7,
                       [T_win[2], T_hT], [T_pb[zb]])
            ul_ = []
            for g in range(4):
                u_, Tu = gbuf()
                ul_.append((u_, Tu))
                act(u_[:, cs_], PB[zub[g]][:, cs_], AF.Gelu_apprx_tanh, [T_pb[zub[g]]], Tu)
            if p3:
                rsqrt(lnr4[:, 3:4], lnmv4[:, 3:4, 1], 1.0, [T_lnmv], T_lnr)
            else:
                rsqrt(lnr4, lnmv4[:, :, 1], 1.0, [T_lnmv], T_lnr)
            for t in TS:
                stt(V, hs[:, t, :], hs[:, t, :], lnmv4[:, t, 0:1], vng, ALU.subtract, ALU.mult,
                    [T_hs[t], T_lnmv, T_vn], [T_hs[t]])
                stt(V, vfull[:, t, :], hs[:, t, :], lnr4[:, t:t + 1], vnb, ALU.mult, ALU.add,
                    [T_hs[t], T_lnr, T_vn2], [T_vf[t]])
            spb = []
            for g in range(4):
                zb = zbank(); spb.append(zb)
                for t in TS:
                    mm(PB[zb][:, t * 128:(t + 1) * 128], vfull[:, t, g * 128:(g + 1) * 128], wsT_sb[:, g, :],
                       True, False, [T_vf[t], T_ws], [T_pb[zb]], inc=False)
                    mm(PB[zb][:, t * 128:(t + 1) * 128], ones_bf[0:1, :], bsp_bf[0:1, g * 128:(g + 1) * 128],
                       False, True, [T_const, T_bsp], [T_pb[zb]], inc=(t == 3))
            for g in range(4):
                u_, Tu = ul_[g]
                tt_(V, ygb[:, g, cs_], PB[spb[g]][:, cs_], u_[:, cs_], ALU.mult, [T_pb[spb[g]], *Tu], [T_ygb[g]])
            for g in range(4):
                act(ysq[g % 2][:, cs_], ygb[:, g, cs_], AF.Square, [T_ygb[g]], [T_ysq[g % 2]])
                ss_mm(g, 16)
            if p3:
                emit(V, lambda e: e.tensor_reduce(out=sl4[:, 3:4], in_=PB[SSB][:, 3:16:4],
                                                  axis=mybir.AxisListType.X, op=ALU.add), [T_pb[SSB]], [T_sl4])
                emit(V, lambda e: e.tensor_reduce(out=sg4[:, 3:4], in_=PB[SSB][:, 19:32:4],
                                                  axis=mybir.AxisListType.X, op=ALU.add), [T_pb[SSB]], [T_sl4])
                rsqrt(rl4[:, 3:4], sl4[:, 3:4], 1.0 / 512, [T_sl4], T_rl4)
                rsqrt(rg4[:, 3:4], sg4[:, 3:4], 1.0 / 512, [T_sl4], T_rg4)
            else:
                emit(V, lambda e: e.tensor_reduce(out=sl4, in_=PB[SSB][:, 0:16].rearrange("p (j t) -> p t j", j=4),
                                                  axis=mybir.AxisListType.X, op=ALU.add), [T_pb[SSB]], [T_sl4])
                emit(V, lambda e: e.tensor_reduce(out=sg4, in_=PB[SSB][:, 16:32].rearrange("p (j t) -> p t j", j=4),
                                                  axis=mybir.AxisListType.X, op=ALU.add), [T_pb[SSB]], [T_sl4])
                rsqrt(rl4, sl4, 1.0 / 512, [T_sl4], T_rl4)
                rsqrt(rg4, sg4, 1.0 / 512, [T_sl4], T_rg4)
            oi = [0]

            def f_evac(t):
                yv, Tyv = ybufs[_yb[0] % 3]
                _yb[0] += 1
                for half in range(2):
                    o1, o2 = OB[oi[0] % 2]
                    oi[0] += 1
                    tob, Tto = tbufs[_tb[0] % len(tbufs)]
                    _tb[0] += 1
                    cs = slice(half * 512, (half + 1) * 512)
                    for j in range(4):
                        mm(PB[o1][:, :], ylb[:, j, t * 128:(t + 1) * 128], W_out_sb[:, j, cs], j == 0, j == 3,
                           [T_ylb[j], T_wouth[0]], [T_pb[o1]])
                    for j in range(4):
                        mm(PB[o2][:, :], ygb[:, j, t * 128:(t + 1) * 128], W_out_sb[:, 4 + j, cs], j == 0, j == 3,
                           [T_ygb[j], T_wouth[1]], [T_pb[o2]])
                    act(tob, PB[o1][:, :], AF.Copy, [T_pb[o1], T_rl4], Tto, scale=rl4[:, t:t + 1])
                    stt(V, yv[:, cs], PB[o2][:, :], rg4[:, t:t + 1], tob, ALU.mult, ALU.add,
                        [T_pb[o2], T_rg4, *Tto], Tyv)
                return yv, Tyv

            def f_post(t, yv, Tyv):
                act(ysq2, yv, AF.Square, Tyv, [T_ysq[0], T_ysq[1], T_ss2c[t]], accum=ss2c[:, t:t + 1])
                rsqrt(rs2c[:, t:t + 1], ss2c[:, t:t + 1], 1.0 / D, [T_ss2c[t]], T_rs2c[t])
                stt(V, yv, yv, rs2c[:, t:t + 1], MGP_m[:, :], ALU.mult, ALU.mult, [*Tyv, T_rs2c[t], TGP_m], Tyv)
                xt = XB[:, tile0 + t, :]
                tt_(V, xt, yv, xt, ALU.add, [*Tyv, T_xb[tile0 + t]], [T_xb[tile0 + t]])

            ev = {TS[0]: f_evac(TS[0])}
            for ti, t in enumerate(TS):
                if ti + 1 < len(TS):
                    ev[TS[ti + 1]] = f_evac(TS[ti + 1])
                f_post(t, *ev[t])
            if p3:
                act(hb, XB[:, tile0 + 3, :], AF.Square, [T_xb[tile0 + 3]], [T_hb, T_ss4], accum=ss4[:, 0:1])
                rsqrt(rstd4[:, 0:1], ss4[:, 0:1], 1.0 / D, [T_ss4], T_rstd4)
                rms_to_hT(tile0 + 3, 0, MG_f, TG_f, MSH_f, TSH_f, hT, T_hT, 512)
                cp(V, h2halo[:, :, :], hT[:, :, 126:128], [T_hT], [T_h2halo])

        w_up_v = w_up_d.rearrange("(k p) n -> p k n", p=128)
        T_wsc = [[T(), T()] for _ in range(24)]

        pending_conv = [(j, gvi) for j in range(24) for gvi in range(2)]

        def pump():
            lst = bg_by_chunk.get(cur_chunk[0])
            if lst:
                lst.pop(0)()
                return
            if not pending_conv:
                return
            j, gvi = pending_conv.pop(0)
            dst = wsc_d[j].rearrange("p (k n) -> p k n", k=8)
            c0 = gvi * 3072 + j * 128
            emit_dma(G, nxt("c", csem),
                     lambda e, dst=dst, gvi=gvi, c0=c0: e.dma_start(out=dst[:, :, gvi * 128:(gvi + 1) * 128],
                                                                  in_=w_up_v[:, :, c0:c0 + 128]),
                     writes=[T_wsc[j][gvi]])

        seq = [("p", c) for c in range(NCHUNK)] + [("m", c) for c in range(NCHUNK)]

        def issue_load(idx):
            kind, c = seq[idx]
            load_x(xp_d if kind == "p" else x_d, c * CH, 4 * c)

        issue_load(0)
        for idx, (kind, c) in enumerate(seq):
            if idx + 1 < len(seq):
                if idx == 0:
                    bg_by_chunk[0].insert(0, lambda: issue_load(1))
                else:
                    issue_load(idx + 1)
            cur_chunk[0] = idx
            if kind == "p":
                mix_chunk(4 * c, full=(c == NCHUNK - 1), p3=(c == NCHUNK - 1))
                while bg_by_chunk.get(idx):
                    bg_by_chunk[idx].pop(0)()
                if c == NCHUNK - 1:
                    for j in range(4):
                        ts(V, carry[:, j:j + 1], carry[:, j:j + 1], flag_sb[:, 0:1], None, ALU.mult, None,
                           [T_carry[j], T_flag], [T_carry[j]])
                        ts(V, halo_x[:, j, :], halo_x[:, j, :], flag_sb[:, 0:1], None, ALU.mult, None,
                           [T_halo_x[j], T_flag], [T_halo_x[j]])
            else:
                mix_chunk(4 * c, full=True)

        while pending_conv:
            pump()
        barrier()

        w_dn_v = w_down_d.rearrange("(j p) n -> p j n", p=128)
        T_wdn = [T() for _ in range(4)]
        for q in range(4):
            gdma(W_dn_sb[:, q * 6:(q + 1) * 6, :], w_dn_v[:, q * 6:(q + 1) * 6, :], writes=[T_wdn[q]])

        cv = Carve()
        t1 = cv.f32(D); T_t1 = T("t1f")
        raw = [cv.f32(516), cv.f32(516)]; T_raw = [T(), T()]
        reg_acc = cv.f32(2048)
        acc = [[reg_acc[:, 0:512], reg_acc[:, 512:1024]], [reg_acc[:, 1024:1536], reg_acc[:, 1536:2048]]]
        T_acc = [[T(), T()] for _ in range(2)]
        reg_gl = cv.f32(1024)
        gl = [reg_gl[:, 0:512], reg_gl[:, 512:1024]]; T_gl = [T(), T()]
        fbufs = [(t1, [T_t1]), (reg_acc[:, 0:1024], T_acc[0]), (reg_acc[:, 1024:2048], T_acc[1])]
        _fbusy = [False, False, False]
        _fnext = [0]

        def ftake():
            for k in range(3):
                i = (_fnext[0] + k) % 3
                if not _fbusy[i]:
                    _fbusy[i] = True
                    _fnext[0] = i + 1
                    return i
            raise RuntimeError("no free FFN row buffer")
        ss2f = sm(4); rs2f = sm(4); T_ss2f = [T() for _ in range(4)]; T_rs2f = [T() for _ in range(4)]
        hb = cv.bf16(1024); T_hb = T("hbf")
        h2T = cv.bf16(4096).rearrange("p (k n) -> p k n", k=8); T_h2T = T("h2T")
        prodT = cv.bf16(12288).rearrange("p (j n) -> p j n", j=24); T_prod = [T() for _ in range(24)]
        wub = [cv.bf16(2048).rearrange("p (k n) -> p k n", k=8) for _ in range(2)]
        wub += [m[:, :].bitcast(BF16).rearrange("p (k n) -> p k n", k=8) for m in (MSH_m, MG_m, MGP_m)]
        NWB = len(wub)
        T_wub = [T() for _ in range(NWB)]
        T_rawh = [T(), T()]

        UPB = [(0, 1), (2, 3)]
        DNB = [4, 5]
        HALB = 6
        def load_wub(bi):
            j = bi % 24
            s = bi % NWB
            emit_dma(Q, nxt("u", usem),
                     lambda e, s=s, j=j: e.dma_start(out=wub[s], in_=wsc_d[j].rearrange("p (k n) -> p k n", k=8)),
                     reads=T_wsc[j], writes=[T_wub[s]])

        fw = lambda k, ch: pcc(40 + k * 48 + ch)
        fb = lambda ch: pcc(184 + ch)

        def f1_stats(fcx):
            for t in range(4):
                act(hb, XB[:, 4 * fcx + t, :], AF.Square, [T_xb[4 * fcx + t]], [T_hb, T_ss4], accum=ss4[:, t:t + 1])
            rsqrt(rstd4[:, 0:4], ss4[:, 0:4], 1.0 / D, [T_ss4], T_rstd4)

        def f1_tile(fcx, t):
            xt = XB[:, 4 * fcx + t, :]
            fi = ftake()
            tb, Ttb = fbufs[fi]
            stt(V, tb, xt, rstd4[:, t:t + 1], MG_f[:, :], ALU.mult, ALU.mult,
                [T_xb[4 * fcx + t], T_rstd4, TG_f], Ttb)
            tt_(V, hb, tb, MSH_f[:, :], ALU.add, [*Ttb, TSH_f], [T_hb])
            _fbusy[fi] = False
            for k in range(8):
                emit(PE, lambda e, k=k, hb=hb: e.transpose(out=TRp[:, k * 128:(k + 1) * 128],
                                                            in_=hb[:, k * 128:(k + 1) * 128], identity=ident[:, :]),
                     [T_hb, T_const], [T_tr], inc=(k == 7))
            act(h2T[:, :, t * 128:(t + 1) * 128], TRp[:, :].rearrange("p (k n) -> p k n", k=8), AF.Copy,
                [T_tr], [T_h2T])

        for bi0 in range(NWB):
            load_wub(bi0)
        upi = 0
        dni = 0
        for fc in range(NCHUNK):
            tile0 = 4 * fc
            if fc == 0:
                f1_stats(0)
                for t in range(4):
                    f1_tile(0, t)
            for j in range(24):
                bi = fc * 24 + j
                s = bi % NWB
                pg, pv = UPB[upi % 2]
                upi += 1
                for gvi, pbk in ((0, pg), (1, pv)):
                    co = gvi * 128
                    for k in range(8):
                        mm(PB[pbk][:, :], wub[s][:, k, co:co + 128], h2T[:, k, :], k == 0, k == 7,
                           [T_wub[s], T_h2T], [T_pb[pbk]])
                if fc == 0:
                    for gvi in (0, 1):
                        co = gvi * 128
                        for k in range(8):
                            mm(PB[HALB][:, 2 * gvi:2 * gvi + 2], wub[s][:, k, co:co + 128], h2halo[:, k, :],
                               k == 0, k == 7, [T_wub[s], T_h2halo], [T_pb[HALB]], inc=(k == 7 and gvi == 1))
                for gvi, pbk in ((0, pg), (1, pv)):
                    ch = gvi * 24 + j
                    rw = raw[gvi]; Tr = T_raw[gvi]
                    ac = acc[upi % 2][gvi]; Ta = T_acc[upi % 2][gvi]
                    Trh = T_rawh[gvi]
                    if fc == 0:
                        act(rw[:, 0:2], PB[HALB][:, 2 * gvi:2 * gvi + 2], AF.Copy, [T_pb[HALB], T_flag], [Trh],
                            scale=flag_sb[:, 0:1])
                    else:
                        act(rw[:, 0:2], halo_f[:, ch, :], AF.Copy, [T_halo_f[ch]], [Trh])
                    act(rw[:, 2:514], PB[pbk][:, :], AF.Copy, [T_pb[pbk]], [Tr])
                    act(ac, PB[pbk][:, :], AF.Identity, [T_pb[pbk], T_pc], [Ta], bias=fb(ch), scale=fw(2, ch))
                    cp(V, halo_f[:, ch, :], rw[:, 512:514], [Tr], [T_halo_f[ch]])
                    stt(V, ac, rw[:, 1:513], fw(1, ch), ac, ALU.mult, ALU.add, [Tr, Trh, T_pc, Ta], [Ta])
                    stt(V, ac, rw[:, 0:512], fw(0, ch), ac, ALU.mult, ALU.add, [Tr, Trh, T_pc, Ta], [Ta])
                g_ = gl[upi % 2]; Tg = T_gl[upi % 2]
                act(g_, acc[upi % 2][0], AF.Gelu_apprx_tanh, [T_acc[upi % 2][0]], [Tg])
                tt_(V, prodT[:, j, :], g_, acc[upi % 2][1], ALU.mult, [Tg, T_acc[upi % 2][1]], [T_prod[j]])
                if bi + NWB < NCHUNK * 24:
                    load_wub(bi + NWB)
            def d_evac(t):
                nonlocal dni
                fi = ftake()
                y2, Ty2 = fbufs[fi]
                for half in range(2):
                    pbk = DNB[dni % 2]
                    dni += 1
                    cs = slice(half * 512, (half + 1) * 512)
                    for j in range(24):
                        mm(PB[pbk][:, :], prodT[:, j, t * 128:(t + 1) * 128], W_dn_sb[:, j, cs], j == 0, j == 23,
                           [T_prod[j], T_wdn[j // 6]], [T_pb[pbk]])
                    act(y2[:, cs], PB[pbk][:, :], AF.Copy, [T_pb[pbk]], Ty2)
                return y2, Ty2, fi

            def d_post(t, y2, Ty2, fi):
                act(reg_gl, y2, AF.Square, Ty2, [T_gl[0], T_gl[1], T_ss2f[t]], accum=ss2f[:, t:t + 1])
                rsqrt(rs2f[:, t:t + 1], ss2f[:, t:t + 1], 1.0 / D, [T_ss2f[t]], T_rs2f[t])
                stt(V, y2, y2, rs2f[:, t:t + 1], MGP_f[:, :], ALU.mult, ALU.mult, [*Ty2, T_rs2f[t], TGP_f], Ty2)
                xt = XB[:, tile0 + t, :]
                tt_(V, xt, y2, xt, ALU.add, [*Ty2, T_xb[tile0 + t]], [T_xb[tile0 + t]])
                _fbusy[fi] = False
                r0 = (tile0 + t) * 128
                emit_dma(Q, nxt("o", osem),
                         lambda e, xt=xt, r0=r0: e.dma_start(out=out_d[r0:r0 + 128, :], in_=xt),
                         reads=[T_xb[tile0 + t]])

            nxt_fc = fc + 1 if fc + 1 < NCHUNK else None
            if nxt_fc is not None:
                f1_stats(nxt_fc)
            dv = {0: d_evac(0)}
            for t in range(4):
                if t + 1 < 4:
                    dv[t + 1] = d_evac(t + 1)
                if nxt_fc is not None:
                    f1_tile(nxt_fc, t)
                d_post(t, *dv[t])

        fin = [(s, s.count) for s in osem if s.count > 0]
        Q.ops.append((fin, lambda e: None, None, 0))

        def rp(E):
            def f(eng):
                for waits, fn, sem, amt in E.ops:
                    for s, v in waits:
                        eng.wait_ge(s.h, v)
                    ins = fn(eng)
                    if sem is not None and ins is not None:
                        ins.then_inc(sem.h, amt)
            return f

        with nc.Block() as block:
            block.tensor(rp(PE))
            block.scalar(rp(A))
            block.vector(rp(V))
            block.gpsimd(rp(G))
            block.sync(rp(Q))
    return nc


_NC_CACHE = {}


def _pack_pcols(inp):
    pcols = np.zeros((128, NPC), np.float32)

    def colmaj(v):
        v = np.asarray(v, np.float32).reshape(-1, 128)
        return v.T

    cw = np.asarray(inp["conv_w"][0], np.float32)
    for k in range(4):
        pcols[:, k * 4:(k + 1) * 4] = colmaj(cw[k])
    pcols[:, 16:20] = colmaj(inp["conv_b"][0])
    pcols[:, 20:24] = colmaj(np.asarray(inp["b_rgate"][0]).reshape(-1))
    pcols[:, 24:28] = colmaj(np.asarray(inp["b_igate"][0]).reshape(-1))
    pcols[:, 28:32] = colmaj(inp["lru_a"][0])
    pcols[:, 32:36] = colmaj(inp["g_lru_out"][0])
    pcols[:, 36:40] = colmaj(inp["g_gmlp_out"][0])
    fcw = np.asarray(inp["ffn_conv_w"][0], np.float32)
    for k in range(3):
        pcols[:, 40 + k * 48:40 + (k + 1) * 48] = colmaj(fcw[k])
    pcols[:, 184:232] = colmaj(inp["ffn_conv_b"][0])
    return pcols


def _block_diag(w):
    w = np.asarray(w, np.float32)
    out = np.zeros((128, 4, 128), np.float32)
    for j in range(4):
        for hl in range(2):
            out[hl * 64:(hl + 1) * 64, j, hl * 64:(hl + 1) * 64] = w[2 * j + hl]
    return out


def kernel(**inp):
    x = np.ascontiguousarray(np.asarray(inp["x"], np.float32))
    c = np.asarray(inp["c"], np.float32)
    if "nc" not in _NC_CACHE:
        _NC_CACHE["nc"] = build_nc()
    nc = _NC_CACHE["nc"]

    f32 = lambda a: np.ascontiguousarray(np.asarray(a, np.float32))
    shared = {
        "w_ada": f32(inp["w_ada"][0]),
        "b_ada": f32(inp["b_ada"][0]),
        "g_mix_pre": f32(inp["g_mix_pre"][0]),
        "g_mix_post": f32(inp["g_mix_post"][0]),
        "g_ffn_pre": f32(inp["g_ffn_pre"][0]),
        "g_ffn_post": f32(inp["g_ffn_post"][0]),
        "w_in": f32(inp["w_in"][0]),
        "pcols": _pack_pcols(inp),
        "w_rgate": _block_diag(inp["w_rgate"][0]),
        "w_igate": _block_diag(inp["w_igate"][0]),
        "v_norm_g": f32(inp["v_norm_g"][0]),
        "v_norm_b": f32(inp["v_norm_b"][0]),
        "wsT": f32(np.transpose(np.asarray(inp["w_spatial"][0]), (0, 2, 1))),
        "b_spatial": f32(np.asarray(inp["b_spatial"][0]).reshape(1, 512)),
        "w_out": f32(inp["w_out"][0]),
        "w_up": f32(inp["w_up"][0]),
        "w_down": f32(inp["w_down"][0]),
    }
    in_maps = []
    for r in range(8):
        b, h = r // 2, r % 2
        m = dict(shared)
        m["x"] = np.ascontiguousarray(x[b, h * NTOK:(h + 1) * NTOK])
        m["xp"] = np.ascontiguousarray(x[b, 0:NTOK])
        m["flag"] = np.full((128, 1), float(h), np.float32)
        m["cvec"] = np.ascontiguousarray(c[b].reshape(8, 128).T)
        in_maps.append(m)
    res = run_bass_kernel_spmd(nc, in_maps, core_ids=list(range(8)))
    out = np.empty((4, 2 * NTOK, D), np.float32)
    for r in range(8):
        b, h = r // 2, r % 2
        out[b, h * NTOK:(h + 1) * NTOK] = res.results[r]["out"]
    return out
```

```python
from contextlib import ExitStack

import numpy as np
import concourse.bass as bass
import concourse.mybir as mybir
from concourse.bass_utils import run_bass_kernel_spmd

F32 = mybir.dt.float32
BF16 = mybir.dt.bfloat16
AF = mybir.ActivationFunctionType
ALU = mybir.AluOpType

D = 1024
NTOK = 2048
CH = 512
NCHUNK = NTOK // CH
EPS = 1e-6
NPC = 232
AR = 16140


class Sem:
    def __init__(self, h):
        self.h = h
        self.count = 0


class Eng:
    def __init__(self, name, sem, is_pe=False):
        self.name = name
        self.sem = sem
        self.ops = []
        self.seen = {}
        self.is_pe = is_pe


class T:
    __slots__ = ("w", "r", "name")

    def __init__(self, name=""):
        self.w = None
        self.r = {}
        self.name = name


def _collect(E, reads, writes):
    needs = {}

    def need(s, v):
        if needs.get(s, 0) < v:
            needs[s] = v

    for t in reads:
        if t.w is not None:
            need(*t.w)
    for t in writes:
        if t.w is not None:
            need(*t.w)
        for s, v in t.r.items():
            need(s, v)
    waits = []
    for s, v in needs.items():
        if s is E.sem:
            if E.is_pe:
                continue
            if v < E.sem.count - 3:
                continue
        if E.seen.get(s, 0) >= v:
            continue
        E.seen[s] = v
        waits.append((s, v))
    return waits


def emit(E, fn, reads=(), writes=(), inc=True):
    waits = _collect(E, reads, writes)
    if inc:
        E.sem.count += 1
        idx = E.sem.count
    else:
        idx = E.sem.count + 1
    E.ops.append((waits, fn, E.sem if inc else None, 1))
    dep = (E.sem, idx)
    for t in writes:
        t.w = dep
        t.r = {}
    for t in reads:
        if t in writes:
            continue
        if t.r.get(E.sem, 0) < idx:
            t.r[E.sem] = idx


def emit_dma(Q, sem, fn, reads=(), writes=()):
    waits = _collect(Q, reads, writes)
    if sem.count > 0 and Q.seen.get(sem, 0) < sem.count:
        Q.seen[sem] = sem.count
        waits.append((sem, sem.count))
    sem.count += 16
    Q.ops.append((waits, fn, sem, 16))
    dep = (sem, sem.count)
    for t in writes:
        t.w = dep
        t.r = {}
    for t in reads:
        if t in writes:
            continue
        if t.r.get(sem, 0) < sem.count:
            t.r[sem] = sem.count


def replay(E, eng):
    for waits, fn, sem, amt in E.ops:
        for s, v in waits:
            eng.wait_ge(s.h, v)
        ins = fn(eng)
        if sem is not None:
            ins.then_inc(sem.h, amt)


def build_nc():
    nc = bass.Bass("TRN2", target_bir_lowering=False)

    def din(name, shape):
        return nc.dram_tensor(name, shape, F32, kind="ExternalInput").ap()

    x_d = din("x", [NTOK, D])
    xp_d = din("xp", [NTOK, D])
    flag_d = din("flag", [128, 1])
    cvec_d = din("cvec", [128, 8])
    w_ada_d = din("w_ada", [D, 6 * D])
    b_ada_d = din("b_ada", [6 * D])
    gpre_d = [din(n, [D]) for n in ("g_mix_pre", "g_mix_post", "g_ffn_pre", "g_ffn_post")]
    w_in_d = din("w_in", [D, 2048])
    pcols_d = din("pcols", [128, NPC])
    wrg_d = din("w_rgate", [128, 4, 128])
    wig_d = din("w_igate", [128, 4, 128])
    vng_d = din("v_norm_g", [512])
    vnb_d = din("v_norm_b", [512])
    wsT_d = din("wsT", [4, 128, 128])
    bsp_d = din("b_spatial", [1, 512])
    w_out_d = din("w_out", [D, D])
    w_up_d = din("w_up", [D, 6144])
    w_down_d = din("w_down", [3072, D])
    out_d = nc.dram_tensor("out", [NTOK, D], F32, kind="ExternalOutput").ap()
    wsc_d = nc.dram_tensor("wup_bf", [24, 128, 2048], BF16, kind="Internal").ap()

    with ExitStack() as es:
        def S(name, shape, dt):
            return es.enter_context(nc.sbuf_tensor(name, shape, dt))

        def PSt(name, shape, dt):
            return es.enter_context(nc.psum_tensor(name, shape, dt))

        def SEM(name):
            return Sem(es.enter_context(nc.semaphore(name)))

        PE = Eng("pe", SEM("s_pe"), is_pe=True)
        A = Eng("act", SEM("s_act"))
        V = Eng("dve", SEM("s_dve"))
        G = Eng("pool", SEM("s_pool"))
        Q = Eng("sp", SEM("s_sp"))
        engines = [PE, A, V, G, Q]

        pc = S("pc", [128, NPC], F32)
        T_pc = T("pc")
        flag_sb = S("flag_sb", [128, 1], F32)
        T_flag = T()
        small = S("small", [128, 128], F32)
        ident = S("ident", [128, 128], BF16)
        ones_bf = S("ones_bf", [128, 128], BF16)
        T_const = T()
        wr_bd = S("wr_bd", [128, 4, 128], BF16)
        wi_bd = S("wi_bd", [128, 4, 128], BF16)
        T_gw = T()
        T_gw2 = T()
        wsT_sb = S("wsT_sb", [128, 4, 128], BF16)
        T_ws = T()
        bsp_bf = S("bsp_bf", [1, 512], BF16)
        T_bsp = T()
        MOD = [S(f"mod{n}", [128, D], F32) for n in range(6)]
        T_mod = [T() for _ in range(6)]
        MSH_m, MG_m, MGP_m, MSH_f, MG_f, MGP_f = MOD
        TSH_m, TG_m, TGP_m, TSH_f, TG_f, TGP_f = T_mod
        WMIX = S("wmix", [128, 24576], BF16)
        T_win = [T() for _ in range(4)]
        T_wouth = [T(), T()]
        W_in_sb = WMIX[:, 0:16384].rearrange("p (k n) -> p k n", k=8)
        W_out_sb = WMIX[:, 16384:24576].rearrange("p (k n) -> p k n", k=8)
        W_dn_sb = WMIX[:, :].rearrange("p (j n) -> p j n", j=24)
        XB = S("xb", [128, 16, D], F32)
        T_xb = [T(f"xb{i}") for i in range(16)]
        halo_f = S("halo_f", [128, 48, 2], F32)
        T_halo_f = [T() for _ in range(48)]
        h2halo = S("h2halo", [128, 8, 2], BF16)
        T_h2halo = T()
        arena = S("arena", [128, AR], F32)

        _so = [0]

        def sm(n):
            ap = small[:, _so[0]:_so[0] + n]
            _so[0] += n
            assert _so[0] <= 128
            return ap

        ss4 = sm(4); T_ss4 = T()
        rstd4 = sm(4); T_rstd4 = T()
        negc = sm(4); negc_h = sm(4)
        brh = sm(4); bih = sm(4)
        T_lp = T()
        carry = sm(4); T_carry = [T() for _ in range(4)]
        halo_x = sm(12).rearrange("p (j c) -> p j c", j=4); T_halo_x = [T() for _ in range(4)]
        lnst = sm(6); T_lnst = T()
        lnmv = sm(2); T_lnmv = T()
        lnr = sm(1); T_lnr = T()
        rl4 = sm(4); T_rl4 = T()
        rg4 = sm(4); T_rg4 = T()
        ss2 = sm(1); T_ss2 = T()
        rs2 = sm(1); T_rs2 = T()
        stmp = sm(16); T_stmp = T()
        cact_f = sm(8); T_cactf = T()
        eps_t = sm(1); one_t = sm(1); T_k = T()
        sl4 = sm(4); sg4 = sm(4); T_sl4 = T()
        lnmv4 = sm(8).rearrange("p (t c) -> p t c", t=4); lnr4 = sm(4)

        def pcc(off, n=1):
            return pc[:, off:off + n]

        PB = [PSt(f"pb{i}", [128, 512], F32) for i in range(7)]
        T_pb = [T(f"pb{i}") for i in range(7)]
        TRp = PSt("trp", [128, 1024], BF16)
        T_tr = T("tr")

        xsem = [SEM(f"xs{i}") for i in range(4)]
        wsem = [SEM(f"ws{i}") for i in range(4)]
        msem = [SEM(f"ms{i}") for i in range(4)]
        osem = [SEM(f"os{i}") for i in range(4)]
        csem = [SEM(f"cs{i}") for i in range(8)]
        usem = [SEM(f"us{i}") for i in range(6)]
        ssem = [SEM(f"ss{i}") for i in range(20)]
        _rr = {"x": 0, "w": 0, "m": 0, "o": 0, "c": 0, "u": 0, "s": 0}

        def nxt(kind, lst):
            i = _rr[kind]
            _rr[kind] = (i + 1) % len(lst)
            return lst[i]

        def qdma(out, in_, reads=(), writes=()):
            emit_dma(Q, nxt("m", msem), lambda e, o=out, i=in_: e.dma_start(out=o, in_=i),
                     reads, writes)

        def gdma(out, in_, reads=(), writes=(), small=False):
            emit_dma(G, nxt("s", ssem) if small else nxt("w", wsem),
                     lambda e, o=out, i=in_: e.dma_start(out=o, in_=i), reads, writes)

        def act(out, in_, func, reads, writes, bias=None, scale=None, accum=None, eng=None):
            kw = {}
            if bias is not None:
                kw["bias"] = bias
            if scale is not None:
                kw["scale"] = scale
            if accum is not None:
                kw["accum_out"] = accum
            emit(A, lambda e: e.activation(out=out, in_=in_, func=func, **kw), reads, writes)

        def ts(E, out, in0, s1, s2, op0, op1, reads, writes):
            if s2 is None:
                emit(E, lambda e: e.tensor_scalar(out=out, in0=in0, scalar1=s1, scalar2=None, op0=op0),
                     reads, writes)
            else:
                emit(E, lambda e: e.tensor_scalar(out=out, in0=in0, scalar1=s1, scalar2=s2, op0=op0, op1=op1),
                     reads, writes)

        def stt(E, out, in0, scalar, in1, op0, op1, reads, writes):
            emit(E, lambda e: e.scalar_tensor_tensor(out=out, in0=in0, scalar=scalar, in1=in1, op0=op0, op1=op1),
                 reads, writes)

        def tt_(E, out, in0, in1, op, reads, writes):
            emit(E, lambda e: e.tensor_tensor(out=out, in0=in0, in1=in1, op=op), reads, writes)

        def cp(E, out, in_, reads, writes):
            emit(E, lambda e: e.tensor_copy(out=out, in_=in_), reads, writes)

        def rsqrt(out, in_, scale, reads, Tout):
            act(out, in_, AF.Sqrt, list(reads) + [T_k], [Tout], bias=eps_t[:, 0:1], scale=scale)
            emit(V, lambda e: e.reciprocal(out=out, in_=out), [Tout], [Tout])

        def mm(out, lhsT, rhs, start, stop, reads, writes, inc=None):
            if inc is None:
                inc = stop
            emit(PE, lambda e: e.matmul(out, lhsT, rhs, start=start, stop=stop), reads, writes, inc=inc)

        def barrier():
            allsems = [E.sem for E in engines] + xsem + wsem + msem + osem + csem + usem + ssem
            for E in engines:
                waits = []
                for s in allsems:
                    if s is E.sem or s.count == 0:
                        continue
                    if E.seen.get(s, 0) < s.count:
                        E.seen[s] = s.count
                        waits.append((s, s.count))
                if waits:
                    E.ops.append((waits, lambda e: None, None, 0))

        class Carve:
            def __init__(self):
                self.off = 0

            def f32(self, n):
                ap = arena[:, self.off:self.off + n]
                self.off += n
                assert self.off <= AR, self.off
                return ap

            def bf16(self, n):
                assert n % 2 == 0
                ap = arena[:, self.off:self.off + n // 2].bitcast(BF16)
                self.off += n // 2
                assert self.off <= AR, self.off
                return ap

        def _mk_slot(t0):
            return (XB[:, t0:t0 + 2, :].rearrange("p a b -> p (a b)").bitcast(BF16).rearrange("p (k n) -> p k n", k=8),
                    [T_xb[t0], T_xb[t0 + 1]])
        wa_sl = {"m": [_mk_slot(12), _mk_slot(14)], "f": [_mk_slot(0), _mk_slot(2)]}
        gtmp = [XB[:, 8, :], XB[:, 9, :], XB[:, 4, :], XB[:, 5, :]]
        T_gtmp = [T_xb[8], T_xb[9], T_xb[4], T_xb[5]]
        cact_bc = S("cact_bc", [128, 8, 128], BF16)
        T_cact = T()

        emit(G, lambda e: e.memset(small[:, :], 0.0), writes=[T_carry[0], T_carry[1], T_carry[2], T_carry[3],
                                                              T_halo_x[0], T_halo_x[1], T_halo_x[2], T_halo_x[3],
                                                              T_cactf, T_lp, T_ss4, T_rstd4, T_k])
        emit(G, lambda e: e.memset(halo_f[:, :, :], 0.0), writes=T_halo_f)
        emit(G, lambda e: e.memset(eps_t, EPS), writes=[T_k])
        emit(G, lambda e: e.memset(one_t, 1.0), writes=[T_k])
        qdma(pc[:, :], pcols_d[:, :], writes=[T_pc])
        qdma(flag_sb[:, :], flag_d[:, :], writes=[T_flag])
        qdma(cact_f, cvec_d[:, :], writes=[T_cactf])

        emit(G, lambda e: e.memset(ident[:, :], 0.0), writes=[T_const])
        emit(G, lambda e: e.affine_select(out=ident[:, :], in_=ident[:, :], compare_op=ALU.not_equal, fill=1.0,
                                          base=0, pattern=[[-1, 128]], channel_multiplier=1), writes=[T_const])
        emit(G, lambda e: e.memset(ones_bf[:, :], 1.0), writes=[T_const])

        wada_v = w_ada_d.rearrange("(k p) n -> p k n", p=128)

        def load_wa(b):
            slot, Tsl = wa_sl["m" if b < 6 else "f"][b % 2]
            n, half = b // 2, b % 2
            gdma(slot, wada_v[:, :, b * 512:(b + 1) * 512], writes=Tsl)
            qdma(MOD[n][:, half * 512:(half + 1) * 512], b_ada_d[b * 512:(b + 1) * 512].partition_broadcast(128),
                 writes=[T_modh[n][half]])

        def comp_wa(b, pbk=None):
            n, half = b // 2, b % 2
            slot, Tsl = wa_sl["m" if b < 6 else "f"][b % 2]
            if pbk is None:
                pbk = b % 2
            for k in range(8):
                mm(PB[pbk][:, :], cact_bc[:, k, :], slot[:, k, :], k == 0, k == 7,
                   [T_cact, *Tsl], [T_pb[pbk]])
            mh = MOD[n][:, half * 512:(half + 1) * 512]
            tt_(V, mh, PB[pbk][:, :], mh, ALU.add, [T_pb[pbk], T_modh[n][half]], [T_modh[n][half], T_mod[n]])

        T_modh = [[T(), T()] for _ in range(6)]
        load_wa(0)
        load_wa(1)
        w_in_v = w_in_d.rearrange("(k p) n -> p k n", p=128)
        act(cact_f, cact_f, AF.Silu, [T_cactf], [T_cactf])
        cp(V, cact_bc[:, :, :], cact_f.unsqueeze(2).to_broadcast([128, 8, 128]), [T_cactf], [T_cact])

        la = pcc(28, 4)
        x_ = stmp[:, 0:4]; ax = stmp[:, 4:8]; ee = stmp[:, 8:12]; mx = stmp[:, 12:16]
        ts(V, x_, la, -1.0, None, ALU.mult, None, [T_pc], [T_stmp])
        tt_(V, ax, x_, la, ALU.max, [T_stmp, T_pc], [T_stmp])
        ts(V, mx, x_, 0.0, None, ALU.max, None, [T_stmp], [T_stmp])
        act(ee, ax, AF.Exp, [T_stmp], [T_stmp], scale=-1.0)
        act(ee, ee, AF.Ln, [T_stmp, T_k], [T_stmp], bias=one_t[:, 0:1])
        tt_(V, ee, ee, mx, ALU.add, [T_stmp], [T_stmp])
        ts(V, negc, ee, -8.0, None, ALU.mult, None, [T_stmp], [T_lp])
        ts(V, negc_h, ee, -4.0, None, ALU.mult, None, [T_stmp], [T_lp])
        ts(V, brh, pcc(20, 4), 0.5, None, ALU.mult, None, [T_pc], [T_lp])
        ts(V, bih, pcc(24, 4), 0.5, None, ALU.mult, None, [T_pc], [T_lp])

        w_out_v = w_out_d.rearrange("(k p) n -> p k n", p=128)

        def _wout_task(hh):
            def f():
                qdma(XB[:, 8:12, :], w_out_v[:, hh * 4:(hh + 1) * 4, :], writes=T_xb[8:12])
                for kk in range(4):
                    k = hh * 4 + kk
                    ts(V, W_out_sb[:, k, :], XB[:, 8 + kk, :], pcc(32 + k), None, ALU.mult, None,
                       [T_xb[8 + kk], T_pc], [T_wouth[hh]])
            return f

        qdma(gtmp[0], gpre_d[0].partition_broadcast(128), writes=[T_gtmp[0]])

        def _gpost_load():
            qdma(gtmp[1], gpre_d[1].partition_broadcast(128), writes=[T_gtmp[1]])

        for b in range(4):
            comp_wa(b)
            load_wa(b + 2)
            if b == 1:
                gdma(W_in_sb[:, :, 0:512], w_in_v[:, :, 0:512], writes=[T_win[0]])
        stt(V, MG_m[:, :], MG_m[:, :], 1.0, gtmp[0], ALU.add, ALU.mult, [TG_m, T_gtmp[0]], [TG_m])
        gdma(wr_bd[:, :, :], wrg_d[:, :, :], writes=[T_gw], small=True)
        gdma(wi_bd[:, :, :], wig_d[:, :, :], writes=[T_gw2], small=True)
        gdma(wsT_sb[:, :, :], wsT_d.rearrange("g j i -> j g i"), writes=[T_ws], small=True)
        emit(G, lambda e: e.memset(wsT_sb[64:128, :, 0:64], 0.0), writes=[T_ws])
        gdma(bsp_bf[:, :], bsp_d[:, :], writes=[T_bsp], small=True)

        def _win_rest():
            for q in range(1, 4):
                gdma(W_in_sb[:, :, q * 512:(q + 1) * 512], w_in_v[:, :, q * 512:(q + 1) * 512], writes=[T_win[q]])


        def _mtask(b):
            def f():
                comp_wa(b, pbk=2)
            return f

        def _m_final():
            tt_(V, MGP_m[:, :], MGP_m[:, :], gtmp[1], ALU.mult, [TGP_m, T_gtmp[1]], [TGP_m])

        def _f_start():
            load_wa(6)
            load_wa(7)

        def _ftask(b):
            def f():
                comp_wa(b, pbk=2)
                if b + 2 < 12:
                    load_wa(b + 2)
            return f

        def _f_gtmp():
            for n in (2, 3):
                qdma(gtmp[n], gpre_d[n].partition_broadcast(128), writes=[T_gtmp[n]])

        def _f_final():
            stt(V, MG_f[:, :], MG_f[:, :], 1.0, gtmp[2], ALU.add, ALU.mult, [TG_f, T_gtmp[2]], [TG_f])
            tt_(V, MGP_f[:, :], MGP_f[:, :], gtmp[3], ALU.mult, [TGP_f, T_gtmp[3]], [TGP_f])

        bg_by_chunk = {
            0: [_gpost_load, _mtask(4), _mtask(5), _m_final, _wout_task(0), _wout_task(1), _win_rest],
            1: [_f_start, _ftask(6), _ftask(7), _ftask(8)],
            2: [_f_gtmp, _ftask(9), _ftask(10), _ftask(11), _f_final],
        }
        cur_chunk = [0]

        _w = [(sm_, sm_.count) for sm_ in ssem if sm_.count > 0 and PE.seen.get(sm_, 0) < sm_.count]
        for sm_, c_ in _w:
            PE.seen[sm_] = c_
        PE.ops.append((_w, lambda e: None, None, 0))

        cv = Carve()
        vng = cv.f32(512); vnb = cv.f32(512); T_vn = T(); T_vn2 = T()
        t1 = cv.f32(D); T_t1 = T("t1")
        zx = cv.f32(516); T_zx = T("zx")
        xc = [cv.f32(512), cv.f32(512)]; T_xc = [T(), T()]
        reg_hs = cv.f32(2048)
        hs = reg_hs.rearrange("p (j n) -> p j n", j=4); T_hs = [T() for _ in range(4)]
        r01 = [cv.f32(512), cv.f32(512)]
        i01 = [cv.f32(512), cv.f32(512)]
        gg = [cv.f32(512) for _ in range(2)]; T_gg = [T(), T()]
        to1 = cv.f32(512); T_to1 = T()
        hb = cv.bf16(1024); T_hb = T("hb")
        hbs = [(hb, T_hb), (cv.bf16(1024), T("hb2"))]
        hT = cv.bf16(4096).rearrange("p (k n) -> p k n", k=8); T_hT = T("hT")
        xcb = cv.bf16(512); T_xcb = T()
        reg_ylb = cv.f32(1024); reg_ygb = cv.f32(1024); reg_vf = cv.f32(1024)
        ylb = reg_ylb.bitcast(BF16).rearrange("p (j n) -> p j n", j=4); T_ylb = [T() for _ in range(4)]
        ygb = reg_ygb.bitcast(BF16).rearrange("p (j n) -> p j n", j=4); T_ygb = [T() for _ in range(4)]
        vfull = reg_vf.bitcast(BF16).rearrange("p (t n) -> p t n", t=4); T_vf = [T() for _ in range(4)]
        ysq2 = cv.bf16(1024); T_ysq = [T(), T()]
        ysq = [ysq2[:, 0:512], ysq2[:, 512:1024]]
        r_ = [r01[0], r01[1], reg_ylb[:, 0:512], reg_ylb[:, 512:1024]]
        T_r = [[T()], [T()], [T_ylb[0], T_ylb[1]], [T_ylb[2], T_ylb[3]]]
        i_ = [i01[0], i01[1], reg_ygb[:, 0:512], reg_ygb[:, 512:1024]]
        T_i = [[T()], [T()], [T_ygb[0], T_ygb[1]], [T_ygb[2], T_ygb[3]]]
        zxs = [zx, reg_vf[:, 0:516]]
        T_zxs = [[T_zx], [T_vf[0], T_vf[1], T_vf[2]]]
        ybufs = [(t1, [T_t1]), (reg_hs[:, 0:1024], [T_hs[0], T_hs[1]]), (reg_hs[:, 1024:2048], [T_hs[2], T_hs[3]])]
        tbufs = [(to1, [T_to1]), (r01[0], T_r[0]), (r01[1], T_r[1]), (i01[0], T_i[0]), (i01[1], T_i[1])]
        ss2c = sm(4); rs2c = sm(4); T_ss2c = [T() for _ in range(4)]; T_rs2c = [T() for _ in range(4)]
        _yb = [0]
        _tb = [0]
        gbufs = [(gg[0], [T_gg[0]]), (gg[1], [T_gg[1]]), (r01[0], T_r[0]), (r01[1], T_r[1]),
                 (i01[0], T_i[0]), (i01[1], T_i[1])]
        _gb = [0]

        def gbuf():
            b = gbufs[_gb[0] % len(gbufs)]
            _gb[0] += 1
            return b

        qdma(vng, vng_d.partition_broadcast(128), writes=[T_vn])
        qdma(vnb, vnb_d.partition_broadcast(128), writes=[T_vn2])

        ZB = [0, 1, 3, 4, 5, 6]
        SSB = 2
        OB = [(3, 4), (5, 6)]
        _zr = [0]

        def zbank():
            b = ZB[_zr[0] % len(ZB)]
            _zr[0] += 1
            return b

        def load_x(src, row0, tile0):
            sem = nxt("x", xsem)
            emit_dma(Q, sem,
                     lambda e: e.dma_start(out=XB[:, tile0:tile0 + 4, :],
                                           in_=src[row0:row0 + CH, :].rearrange("(t p) d -> p t d", p=128)),
                     writes=T_xb[tile0:tile0 + 4])

        def rms_to_hT(tile_idx, tcol, MGt, TGt, MSHt, TSHt, dstT, T_dst, ncol):
            xt = XB[:, tile_idx, :]
            tb, Ttb = ybufs[_yb[0] % 3]
            _yb[0] += 1
            stt(V, tb, xt, rstd4[:, tcol:tcol + 1], MGt[:, :], ALU.mult, ALU.mult,
                [T_xb[tile_idx], T_rstd4, TGt], Ttb)
            hbx, Thbx = hbs[tcol % 2]
            tt_(G if tcol % 2 == 0 else V, hbx, tb, MSHt[:, :], ALU.add, [*Ttb, TSHt], [Thbx])
            for k in range(8):
                emit(PE, lambda e, k=k, hb=hbx: e.transpose(out=TRp[:, k * 128:(k + 1) * 128],
                                                             in_=hb[:, k * 128:(k + 1) * 128], identity=ident[:, :]),
                     [Thbx, T_const], [T_tr], inc=(k == 7))
            act(dstT[:, :, tcol * 128:(tcol + 1) * 128], TRp[:, :].rearrange("p (k n) -> p k n", k=8), AF.Copy,
                [T_tr], [T_dst])

        def rms_stats(tile0, ntile):
            for t in range(ntile):
                act(hbs[t % 2][0], XB[:, tile0 + t, :], AF.Square, [T_xb[tile0 + t]], [hbs[t % 2][1], T_ss4],
                    accum=ss4[:, t:t + 1])
            rsqrt(rstd4[:, 0:ntile], ss4[:, 0:ntile], 1.0 / D, [T_ss4], T_rstd4)

        def mix_chunk(tile0, full, p3=False):
            rms_stats(tile0, 4)
            for t in range(4):
                rms_to_hT(tile0 + t, t, MG_m, TG_m, MSH_m, TSH_m, hT, T_hT, 512)
            ZXB = [0, 1, 3, 4]
            RIB = [(5, 6), (0, 1), (5, 6), (3, 4)]
            for j in range(4):
                zb = ZXB[j]
                for k in range(8):
                    mm(PB[zb][:, :], W_in_sb[:, k, j * 128:(j + 1) * 128], hT[:, k, :], k == 0, k == 7,
                       [T_win[0], T_hT], [T_pb[zb]])

            def s1(j):
                pump()
                zb = ZXB[j]
                zxj = zxs[j % 2]; Tzx = T_zxs[j % 2]
                cp(G, zxj[:, 0:3], halo_x[:, j, :], [T_halo_x[j]], Tzx)
                act(zxj[:, 3:515], PB[zb][:, :], AF.Copy, [T_pb[zb]], Tzx)
                cp(G, halo_x[:, j, :], zxj[:, 512:515], Tzx, [T_halo_x[j]])
                xcj = xc[j % 2]; Txc = T_xc[j % 2]
                ts(V, xcj, zxj[:, 3:515], pcc(3 * 4 + j), pcc(16 + j), ALU.mult, ALU.add, [*Tzx, T_pc], [Txc])
                for kk in (2, 1, 0):
                    stt(V, xcj, zxj[:, kk:kk + 512], pcc(kk * 4 + j), xcj, ALU.mult, ALU.add,
                        [*Tzx, T_pc, Txc], [Txc])
                cp(G, xcb, xcj, [Txc], [T_xcb])
                zr, zi = RIB[j]
                mm(PB[zr][:, :], wr_bd[:, j, :], xcb, True, True, [T_gw, T_xcb], [T_pb[zr]])
                mm(PB[zi][:, :], wi_bd[:, j, :], xcb, True, True, [T_gw2, T_xcb], [T_pb[zi]])
                return zr, zi

            def s2a(j, zr, zi):
                pump()
                xcj = xc[j % 2]; Txc = T_xc[j % 2]
                rj = r_[j]; ij = i_[j]; Tr_ = T_r[j]; Ti_ = T_i[j]
                act(rj, PB[zr][:, :], AF.Tanh, [T_pb[zr], T_lp], Tr_, bias=brh[:, j:j + 1], scale=0.5)
                act(ij, PB[zi][:, :], AF.Tanh, [T_pb[zi], T_lp], Ti_, bias=bih[:, j:j + 1], scale=0.5)
                act(hs[:, j, :], rj, AF.Exp, [*Tr_, T_lp], [T_hs[j]], bias=negc[:, j:j + 1], scale=negc[:, j:j + 1])
                act(rj, rj, AF.Exp, [*Tr_, T_lp], Tr_, bias=negc_h[:, j:j + 1], scale=negc_h[:, j:j + 1])
                stt(V, ij, ij, 1.0, xcj, ALU.add, ALU.mult, [*Ti_, Txc], Ti_)

            def s2b(j):
                rj = r_[j]; ij = i_[j]; Tr_ = T_r[j]; Ti_ = T_i[j]
                act(hs[:, j, :], hs[:, j, :], AF.Sqrt, [T_hs[j], T_k], [T_hs[j]], bias=one_t[:, 0:1], scale=-1.0)
                stt(V, ij, ij, 0.5, hs[:, j, :], ALU.mult, ALU.mult, [*Ti_, T_hs[j]], Ti_)
                emit(V, lambda e, j=j, hs=hs, rj=rj, ij=ij: e.tensor_tensor_scan(
                    out=hs[:, j, :], data0=rj, data1=ij, initial=carry[:, j:j + 1], op0=ALU.mult, op1=ALU.add),
                     [*Tr_, *Ti_, T_carry[j]], [T_hs[j]])
                cp(G, carry[:, j:j + 1], hs[:, j, 511:512], [T_hs[j]], [T_carry[j]])

            zbk = {}
            for jp in (0, 2):
                zbk[jp] = s1(jp)
                zbk[jp + 1] = s1(jp + 1)
                if jp == 2:
                    s2b(0)
                    s2b(1)
                s2a(jp, *zbk[jp])
                s2a(jp + 1, *zbk[jp + 1])
            s2b(2)
            s2b(3)
            if not full:
                return

            TS = [3] if p3 else [0, 1, 2, 3]
            cs_ = slice(384, 512) if p3 else slice(0, 512)

            def ss_mm(jj, base):
                yq = ysq[jj % 2]; Tyq = T_ysq[jj % 2]
                for t in TS:
                    c0 = base + jj * 4 + t
                    mm(PB[SSB][:, c0:c0 + 1], yq[:, t * 128:(t + 1) * 128], ones_bf[:, 0:1],
                       True, True, [Tyq, T_const], [T_pb[SSB]], inc=(t == 3))

            zgb = []
            for j in range(4):
                zb = zbank(); zgb.append(zb)
                for k in range(8):
                    mm(PB[zb][:, :], W_in_sb[:, k, 512 + j * 128:512 + (j + 1) * 128], hT[:, k, :], k == 0, k == 7,
                       [T_win[1], T_hT], [T_pb[zb]])

            def zv_mm(t):
                zb = zbank(); zvb[t] = zb
                for k in range(8):
                    mm(PB[zb][:, :], hT[:, k, t * 128:(t + 1) * 128], W_in_sb[:, k, 1536:2048], k == 0, k == 7,
                       [T_win[3], T_hT], [T_pb[zb]])

            zvb = {}
            for t in TS[:2]:
                zv_mm(t)
            gl_ = []
            for j in range(4):
                g_, Tg = gbuf()
                gl_.append((g_, Tg))
                act(g_[:, cs_], PB[zgb[j]][:, cs_], AF.Gelu_apprx_tanh, [T_pb[zgb[j]]], Tg)
            for j in range(4):
                g_, Tg = gl_[j]
                tt_(V, ylb[:, j, cs_], hs[:, j, cs_], g_[:, cs_], ALU.mult, [T_hs[j], *Tg], [T_ylb[j]])
            for j in range(4):
                act(ysq[j % 2][:, cs_], ylb[:, j, cs_], AF.Square, [T_ylb[j]], [T_ysq[j % 2]])
                ss_mm(j, 0)
            for t in TS[2:]:
                zv_mm(t)
            for t in TS:
                act(hs[:, t, :], PB[zvb[t]][:, :], AF.Gelu_apprx_tanh, [T_pb[zvb[t]]], [T_hs[t]])
                emit(V, lambda e, t=t, hs=hs: e.bn_stats(out=lnst, in_=hs[:, t, :]), [T_hs[t]], [T_lnst])
                emit(V, lambda e, t=t: e.bn_aggr(out=lnmv4[:, t, :], in_=lnst), [T_lnst], [T_lnmv])
            zub = []
            for g in range(4):
                zb = zbank(); zub.append(zb)
                for k in range(8):
                    mm(PB[zb][:, :], W_in_sb[:, k, 1024 + g * 128:1024 + (g + 1) * 128], hT[:, k, :], k == 0, k == 7,
                       [T_win[2], T_hT], [T_pb[zb]])
            ul_ = []
            for g in range(4):
                u_, Tu = gbuf()
                ul_.append((u_, Tu))
                act(u_[:, cs_], PB[zub[g]][:, cs_], AF.Gelu_apprx_tanh, [T_pb[zub[g]]], Tu)
            if p3:
                rsqrt(lnr4[:, 3:4], lnmv4[:, 3:4, 1], 1.0, [T_lnmv], T_lnr)
            else:
                rsqrt(lnr4, lnmv4[:, :, 1], 1.0, [T_lnmv], T_lnr)
            for t in TS:
                stt(V, hs[:, t, :], hs[:, t, :], lnmv4[:, t, 0:1], vng, ALU.subtract, ALU.mult,
                    [T_hs[t], T_lnmv, T_vn], [T_hs[t]])
                stt(V, vfull[:, t, :], hs[:, t, :], lnr4[:, t:t + 1], vnb, ALU.mult, ALU.add,
                    [T_hs[t], T_lnr, T_vn2], [T_vf[t]])
            spb = []
            for g in range(4):
                zb = zbank(); spb.append(zb)
                for t in TS:
                    mm(PB[zb][:, t * 128:(t + 1) * 128], vfull[:, t, g * 128:(g + 1) * 128], wsT_sb[:, g, :],
                       True, False, [T_vf[t], T_ws], [T_pb[zb]], inc=False)
                    mm(PB[zb][:, t * 128:(t + 1) * 128], ones_bf[0:1, :], bsp_bf[0:1, g * 128:(g + 1) * 128],
                       False, True, [T_const, T_bsp], [T_pb[zb]], inc=(t == 3))
            for g in range(4):
                u_, Tu = ul_[g]
                tt_(V, ygb[:, g, cs_], PB[spb[g]][:, cs_], u_[:, cs_], ALU.mult, [T_pb[spb[g]], *Tu], [T_ygb[g]])
            for g in range(4):
                act(ysq[g % 2][:, cs_], ygb[:, g, cs_], AF.Square, [T_ygb[g]], [T_ysq[g % 2]])
                ss_mm(g, 16)
            if p3:
                emit(V, lambda e: e.tensor_reduce(out=sl4[:, 3:4], in_=PB[SSB][:, 3:16:4],
                                                  axis=mybir.AxisListType.X, op=ALU.add), [T_pb[SSB]], [T_sl4])
                emit(V, lambda e: e.tensor_reduce(out=sg4[:, 3:4], in_=PB[SSB][:, 19:32:4],
                                                  axis=mybir.AxisListType.X, op=ALU.add), [T_pb[SSB]], [T_sl4])
                rsqrt(rl4[:, 3:4], sl4[:, 3:4], 1.0 / 512, [T_sl4], T_rl4)
                rsqrt(rg4[:, 3:4], sg4[:, 3:4], 1.0 / 512, [T_sl4], T_rg4)
            else:
                emit(V, lambda e: e.tensor_reduce(out=sl4, in_=PB[SSB][:, 0:16].rearrange("p (j t) -> p t j", j=4),
                                                  axis=mybir.AxisListType.X, op=ALU.add), [T_pb[SSB]], [T_sl4])
                emit(V, lambda e: e.tensor_reduce(out=sg4, in_=PB[SSB][:, 16:32].rearrange("p (j t) -> p t j", j=4),
                                                  axis=mybir.AxisListType.X, op=ALU.add), [T_pb[SSB]], [T_sl4])
                rsqrt(rl4, sl4, 1.0 / 512, [T_sl4], T_rl4)
                rsqrt(rg4, sg4, 1.0 / 512, [T_sl4], T_rg4)
            oi = [0]

            def f_evac(t):
                yv, Tyv = ybufs[_yb[0] % 3]
                _yb[0] += 1
                for half in range(2):
                    o1, o2 = OB[oi[0] % 2]
                    oi[0] += 1
                    tob, Tto = tbufs[_tb[0] % len(tbufs)]
                    _tb[0] += 1
                    cs = slice(half * 512, (half + 1) * 512)
                    for j in range(4):
                        mm(PB[o1][:, :], ylb[:, j, t * 128:(t + 1) * 128], W_out_sb[:, j, cs], j == 0, j == 3,
                           [T_ylb[j], T_wouth[0]], [T_pb[o1]])
                    for j in range(4):
                        mm(PB[o2][:, :], ygb[:, j, t * 128:(t + 1) * 128], W_out_sb[:, 4 + j, cs], j == 0, j == 3,
                           [T_ygb[j], T_wouth[1]], [T_pb[o2]])
                    act(tob, PB[o1][:, :], AF.Copy, [T_pb[o1], T_rl4], Tto, scale=rl4[:, t:t + 1])
                    stt(V, yv[:, cs], PB[o2][:, :], rg4[:, t:t + 1], tob, ALU.mult, ALU.add,
                        [T_pb[o2], T_rg4, *Tto], Tyv)
                return yv, Tyv

            def f_post(t, yv, Tyv):
                act(ysq2, yv, AF.Square, Tyv, [T_ysq[0], T_ysq[1], T_ss2c[t]], accum=ss2c[:, t:t + 1])
                rsqrt(rs2c[:, t:t + 1], ss2c[:, t:t + 1], 1.0 / D, [T_ss2c[t]], T_rs2c[t])
                stt(V, yv, yv, rs2c[:, t:t + 1], MGP_m[:, :], ALU.mult, ALU.mult, [*Tyv, T_rs2c[t], TGP_m], Tyv)
                xt = XB[:, tile0 + t, :]
                tt_(V, xt, yv, xt, ALU.add, [*Tyv, T_xb[tile0 + t]], [T_xb[tile0 + t]])

            ev = {TS[0]: f_evac(TS[0])}
            for ti, t in enumerate(TS):
                if ti + 1 < len(TS):
                    ev[TS[ti + 1]] = f_evac(TS[ti + 1])
                f_post(t, *ev[t])
            if p3:
                act(hb, XB[:, tile0 + 3, :], AF.Square, [T_xb[tile0 + 3]], [T_hb, T_ss4], accum=ss4[:, 0:1])
                rsqrt(rstd4[:, 0:1], ss4[:, 0:1], 1.0 / D, [T_ss4], T_rstd4)
                rms_to_hT(tile0 + 3, 0, MG_f, TG_f, MSH_f, TSH_f, hT, T_hT, 512)
                cp(V, h2halo[:, :, :], hT[:, :, 126:128], [T_hT], [T_h2halo])

        w_up_v = w_up_d.rearrange("(k p) n -> p k n", p=128)
        T_wsc = [[T(), T()] for _ in range(24)]

        pending_conv = [(j, gvi) for j in range(24) for gvi in range(2)]

        def pump():
            lst = bg_by_chunk.get(cur_chunk[0])
            if lst:
                lst.pop(0)()
                return
            if not pending_conv:
                return
            j, gvi = pending_conv.pop(0)
            dst = wsc_d[j].rearrange("p (k n) -> p k n", k=8)
            c0 = gvi * 3072 + j * 128
            emit_dma(G, nxt("c", csem),
                     lambda e, dst=dst, gvi=gvi, c0=c0: e.dma_start(out=dst[:, :, gvi * 128:(gvi + 1) * 128],
                                                                  in_=w_up_v[:, :, c0:c0 + 128]),
                     writes=[T_wsc[j][gvi]])

        seq = [("p", c) for c in range(NCHUNK)] + [("m", c) for c in range(NCHUNK)]

        def issue_load(idx):
            kind, c = seq[idx]
            load_x(xp_d if kind == "p" else x_d, c * CH, 4 * c)

        issue_load(0)
        for idx, (kind, c) in enumerate(seq):
            if idx + 1 < len(seq):
                if idx == 0:
                    bg_by_chunk[0].insert(0, lambda: issue_load(1))
                else:
                    issue_load(idx + 1)
            cur_chunk[0] = idx
            if kind == "p":
                mix_chunk(4 * c, full=(c == NCHUNK - 1), p3=(c == NCHUNK - 1))
                while bg_by_chunk.get(idx):
                    bg_by_chunk[idx].pop(0)()
                if c == NCHUNK - 1:
                    for j in range(4):
                        ts(V, carry[:, j:j + 1], carry[:, j:j + 1], flag_sb[:, 0:1], None, ALU.mult, None,
                           [T_carry[j], T_flag], [T_carry[j]])
                        ts(V, halo_x[:, j, :], halo_x[:, j, :], flag_sb[:, 0:1], None, ALU.mult, None,
                           [T_halo_x[j], T_flag], [T_halo_x[j]])
            else:
                mix_chunk(4 * c, full=True)

        while pending_conv:
            pump()
        barrier()

        w_dn_v = w_down_d.rearrange("(j p) n -> p j n", p=128)
        T_wdn = [T() for _ in range(4)]
        for q in range(4):
            gdma(W_dn_sb[:, q * 6:(q + 1) * 6, :], w_dn_v[:, q * 6:(q + 1) * 6, :], writes=[T_wdn[q]])

        cv = Carve()
        t1 = cv.f32(D); T_t1 = T("t1f")
        raw = [cv.f32(516), cv.f32(516)]; T_raw = [T(), T()]
        reg_acc = cv.f32(2048)
        acc = [[reg_acc[:, 0:512], reg_acc[:, 512:1024]], [reg_acc[:, 1024:1536], reg_acc[:, 1536:2048]]]
        T_acc = [[T(), T()] for _ in range(2)]
        reg_gl = cv.f32(1024)
        gl = [reg_gl[:, 0:512], reg_gl[:, 512:1024]]; T_gl = [T(), T()]
        fbufs = [(t1, [T_t1]), (reg_acc[:, 0:1024], T_acc[0]), (reg_acc[:, 1024:2048], T_acc[1])]
        _fbusy = [False, False, False]
        _fnext = [0]

        def ftake():
            for k in range(3):
                i = (_fnext[0] + k) % 3
                if not _fbusy[i]:
                    _fbusy[i] = True
                    _fnext[0] = i + 1
                    return i
            raise RuntimeError("no free FFN row buffer")
        ss2f = sm(4); rs2f = sm(4); T_ss2f = [T() for _ in range(4)]; T_rs2f = [T() for _ in range(4)]
        hb = cv.bf16(1024); T_hb = T("hbf")
        h2T = cv.bf16(4096).rearrange("p (k n) -> p k n", k=8); T_h2T = T("h2T")
        prodT = cv.bf16(12288).rearrange("p (j n) -> p j n", j=24); T_prod = [T() for _ in range(24)]
        wub = [cv.bf16(2048).rearrange("p (k n) -> p k n", k=8) for _ in range(2)]
        wub += [m[:, :].bitcast(BF16).rearrange("p (k n) -> p k n", k=8) for m in (MSH_m, MG_m, MGP_m)]
        NWB = len(wub)
        T_wub = [T() for _ in range(NWB)]
        T_rawh = [T(), T()]

        UPB = [(0, 1), (2, 3)]
        DNB = [4, 5]
        HALB = 6
        def load_wub(bi):
            j = bi % 24
            s = bi % NWB
            emit_dma(Q, nxt("u", usem),
                     lambda e, s=s, j=j: e.dma_start(out=wub[s], in_=wsc_d[j].rearrange("p (k n) -> p k n", k=8)),
                     reads=T_wsc[j], writes=[T_wub[s]])

        fw = lambda k, ch: pcc(40 + k * 48 + ch)
        fb = lambda ch: pcc(184 + ch)

        def f1_stats(fcx):
            for t in range(4):
                act(hb, XB[:, 4 * fcx + t, :], AF.Square, [T_xb[4 * fcx + t]], [T_hb, T_ss4], accum=ss4[:, t:t + 1])
            rsqrt(rstd4[:, 0:4], ss4[:, 0:4], 1.0 / D, [T_ss4], T_rstd4)

        def f1_tile(fcx, t):
            xt = XB[:, 4 * fcx + t, :]
            fi = ftake()
            tb, Ttb = fbufs[fi]
            stt(V, tb, xt, rstd4[:, t:t + 1], MG_f[:, :], ALU.mult, ALU.mult,
                [T_xb[4 * fcx + t], T_rstd4, TG_f], Ttb)
            tt_(V, hb, tb, MSH_f[:, :], ALU.add, [*Ttb, TSH_f], [T_hb])
            _fbusy[fi] = False
            for k in range(8):
                emit(PE, lambda e, k=k, hb=hb: e.transpose(out=TRp[:, k * 128:(k + 1) * 128],
                                                            in_=hb[:, k * 128:(k + 1) * 128], identity=ident[:, :]),
                     [T_hb, T_const], [T_tr], inc=(k == 7))
            act(h2T[:, :, t * 128:(t + 1) * 128], TRp[:, :].rearrange("p (k n) -> p k n", k=8), AF.Copy,
                [T_tr], [T_h2T])

        for bi0 in range(NWB):
            load_wub(bi0)
        upi = 0
        dni = 0
        for fc in range(NCHUNK):
            tile0 = 4 * fc
            if fc == 0:
                f1_stats(0)
                for t in range(4):
                    f1_tile(0, t)
            for j in range(24):
                bi = fc * 24 + j
                s = bi % NWB
                pg, pv = UPB[upi % 2]
                upi += 1
                for gvi, pbk in ((0, pg), (1, pv)):
                    co = gvi * 128
                    for k in range(8):
                        mm(PB[pbk][:, :], wub[s][:, k, co:co + 128], h2T[:, k, :], k == 0, k == 7,
                           [T_wub[s], T_h2T], [T_pb[pbk]])
                if fc == 0:
                    for gvi in (0, 1):
                        co = gvi * 128
                        for k in range(8):
                            mm(PB[HALB][:, 2 * gvi:2 * gvi + 2], wub[s][:, k, co:co + 128], h2halo[:, k, :],
                               k == 0, k == 7, [T_wub[s], T_h2halo], [T_pb[HALB]], inc=(k == 7 and gvi == 1))
                for gvi, pbk in ((0, pg), (1, pv)):
                    ch = gvi * 24 + j
                    rw = raw[gvi]; Tr = T_raw[gvi]
                    ac = acc[upi % 2][gvi]; Ta = T_acc[upi % 2][gvi]
                    Trh = T_rawh[gvi]
                    if fc == 0:
                        act(rw[:, 0:2], PB[HALB][:, 2 * gvi:2 * gvi + 2], AF.Copy, [T_pb[HALB], T_flag], [Trh],
                            scale=flag_sb[:, 0:1])
                    else:
                        act(rw[:, 0:2], halo_f[:, ch, :], AF.Copy, [T_halo_f[ch]], [Trh])
                    act(rw[:, 2:514], PB[pbk][:, :], AF.Copy, [T_pb[pbk]], [Tr])
                    act(ac, PB[pbk][:, :], AF.Identity, [T_pb[pbk], T_pc], [Ta], bias=fb(ch), scale=fw(2, ch))
                    cp(V, halo_f[:, ch, :], rw[:, 512:514], [Tr], [T_halo_f[ch]])
                    stt(V, ac, rw[:, 1:513], fw(1, ch), ac, ALU.mult, ALU.add, [Tr, Trh, T_pc, Ta], [Ta])
                    stt(V, ac, rw[:, 0:512], fw(0, ch), ac, ALU.mult, ALU.add, [Tr, Trh, T_pc, Ta], [Ta])
                g_ = gl[upi % 2]; Tg = T_gl[upi % 2]
                act(g_, acc[upi % 2][0], AF.Gelu_apprx_tanh, [T_acc[upi % 2][0]], [Tg])
                tt_(V, prodT[:, j, :], g_, acc[upi % 2][1], ALU.mult, [Tg, T_acc[upi % 2][1]], [T_prod[j]])
                if bi + NWB < NCHUNK * 24:
                    load_wub(bi + NWB)
            def d_evac(t):
                nonlocal dni
                fi = ftake()
                y2, Ty2 = fbufs[fi]
                for half in range(2):
                    pbk = DNB[dni % 2]
                    dni += 1
                    cs = slice(half * 512, (half + 1) * 512)
                    for j in range(24):
                        mm(PB[pbk][:, :], prodT[:, j, t * 128:(t + 1) * 128], W_dn_sb[:, j, cs], j == 0, j == 23,
                           [T_prod[j], T_wdn[j // 6]], [T_pb[pbk]])
                    act(y2[:, cs], PB[pbk][:, :], AF.Copy, [T_pb[pbk]], Ty2)
                return y2, Ty2, fi

            def d_post(t, y2, Ty2, fi):
                act(reg_gl, y2, AF.Square, Ty2, [T_gl[0], T_gl[1], T_ss2f[t]], accum=ss2f[:, t:t + 1])
                rsqrt(rs2f[:, t:t + 1], ss2f[:, t:t + 1], 1.0 / D, [T_ss2f[t]], T_rs2f[t])
                stt(V, y2, y2, rs2f[:, t:t + 1], MGP_f[:, :], ALU.mult, ALU.mult, [*Ty2, T_rs2f[t], TGP_f], Ty2)
                xt = XB[:, tile0 + t, :]
                tt_(V, xt, y2, xt, ALU.add, [*Ty2, T_xb[tile0 + t]], [T_xb[tile0 + t]])
                _fbusy[fi] = False
                r0 = (tile0 + t) * 128
                emit_dma(Q, nxt("o", osem),
                         lambda e, xt=xt, r0=r0: e.dma_start(out=out_d[r0:r0 + 128, :], in_=xt),
                         reads=[T_xb[tile0 + t]])

            nxt_fc = fc + 1 if fc + 1 < NCHUNK else None
            if nxt_fc is not None:
                f1_stats(nxt_fc)
            dv = {0: d_evac(0)}
            for t in range(4):
                if t + 1 < 4:
                    dv[t + 1] = d_evac(t + 1)
                if nxt_fc is not None:
                    f1_tile(nxt_fc, t)
                d_post(t, *dv[t])

        fin = [(s, s.count) for s in osem if s.count > 0]
        Q.ops.append((fin, lambda e: None, None, 0))

        def rp(E):
            def f(eng):
                for waits, fn, sem, amt in E.ops:
                    for s, v in waits:
                        eng.wait_ge(s.h, v)
                    ins = fn(eng)
                    if sem is not None and ins is not None:
                        ins.then_inc(sem.h, amt)
            return f

        with nc.Block() as block:
            block.tensor(rp(PE))
            block.scalar(rp(A))
            block.vector(rp(V))
            block.gpsimd(rp(G))
            block.sync(rp(Q))
    return nc


_NC_CACHE = {}


def _pack_pcols(inp):
    pcols = np.zeros((128, NPC), np.float32)

    def colmaj(v):
        v = np.asarray(v, np.float32).reshape(-1, 128)
        return v.T

    cw = np.asarray(inp["conv_w"][0], np.float32)
    for k in range(4):
        pcols[:, k * 4:(k + 1) * 4] = colmaj(cw[k])
    pcols[:, 16:20] = colmaj(inp["conv_b"][0])
    pcols[:, 20:24] = colmaj(np.asarray(inp["b_rgate"][0]).reshape(-1))
    pcols[:, 24:28] = colmaj(np.asarray(inp["b_igate"][0]).reshape(-1))
    pcols[:, 28:32] = colmaj(inp["lru_a"][0])
    pcols[:, 32:36] = colmaj(inp["g_lru_out"][0])
    pcols[:, 36:40] = colmaj(inp["g_gmlp_out"][0])
    fcw = np.asarray(inp["ffn_conv_w"][0], np.float32)
    for k in range(3):
        pcols[:, 40 + k * 48:40 + (k + 1) * 48] = colmaj(fcw[k])
    pcols[:, 184:232] = colmaj(inp["ffn_conv_b"][0])
    return pcols


def _block_diag(w):
    w = np.asarray(w, np.float32)
    out = np.zeros((128, 4, 128), np.float32)
    for j in range(4):
        for hl in range(2):
            out[hl * 64:(hl + 1) * 64, j, hl * 64:(hl + 1) * 64] = w[2 * j + hl]
    return out


def kernel(**inp):
    x = np.ascontiguousarray(np.asarray(inp["x"], np.float32))
    c = np.asarray(inp["c"], np.float32)
    if "nc" not in _NC_CACHE:
        _NC_CACHE["nc"] = build_nc()
    nc = _NC_CACHE["nc"]

    f32 = lambda a: np.ascontiguousarray(np.asarray(a, np.float32))
    shared = {
        "w_ada": f32(inp["w_ada"][0]),
        "b_ada": f32(inp["b_ada"][0]),
        "g_mix_pre": f32(inp["g_mix_pre"][0]),
        "g_mix_post": f32(inp["g_mix_post"][0]),
        "g_ffn_pre": f32(inp["g_ffn_pre"][0]),
        "g_ffn_post": f32(inp["g_ffn_post"][0]),
        "w_in": f32(inp["w_in"][0]),
        "pcols": _pack_pcols(inp),
        "w_rgate": _block_diag(inp["w_rgate"][0]),
        "w_igate": _block_diag(inp["w_igate"][0]),
        "v_norm_g": f32(inp["v_norm_g"][0]),
        "v_norm_b": f32(inp["v_norm_b"][0]),
        "wsT": f32(np.transpose(np.asarray(inp["w_spatial"][0]), (0, 2, 1))),
        "b_spatial": f32(np.asarray(inp["b_spatial"][0]).reshape(1, 512)),
        "w_out": f32(inp["w_out"][0]),
        "w_up": f32(inp["w_up"][0]),
        "w_down": f32(inp["w_down"][0]),
    }
    in_maps = []
    for r in range(8):
        b, h = r // 2, r % 2
        m = dict(shared)
        m["x"] = np.ascontiguousarray(x[b, h * NTOK:(h + 1) * NTOK])
        m["xp"] = np.ascontiguousarray(x[b, 0:NTOK])
        m["flag"] = np.full((128, 1), float(h), np.float32)
        m["cvec"] = np.ascontiguousarray(c[b].reshape(8, 128).T)
        in_maps.append(m)
    res = run_bass_kernel_spmd(nc, in_maps, core_ids=list(range(8)))
    out = np.empty((4, 2 * NTOK, D), np.float32)
    for r in range(8):
        b, h = r // 2, r % 2
        out[b, h * NTOK:(h + 1) * NTOK] = res.results[r]["out"]
    return out
```
